# Optimizing a Trainium2 kernel written in Bass

```python
import jax, jax.numpy as jnp
from jax import lax
import numpy as np

D_MODEL = 2048
BATCH = 4
SEQ = 4096
DEPTH = 4

CTX_LEN = 256
GRID_W = 64
HEAD_DIM = 128
ATTN_HEADS = 8
KV_HEADS = 2
GQ = ATTN_HEADS // KV_HEADS
ATTN_W = ATTN_HEADS * HEAD_DIM
KV_W = KV_HEADS * HEAD_DIM
WINDOW = 128
BLOCK = 128
ROPE_BASE = 10000.0
CONV_CH = D_MODEL // 2
CONV_K = 31
MIX_W = ATTN_W + CONV_CH
IN_W = ATTN_W + 2 * KV_W + 2 * CONV_CH
D_FF = 4 * D_MODEL
EPS = 1e-6

kernel_name = 'hybrid_window_gqa_conformer_conv_dit_block'


def _rmsnorm(x, g):
    x32 = x.astype(jnp.float32)
    y = x32 * lax.rsqrt(jnp.mean(x32 * x32, axis=-1, keepdims=True) + EPS)
    return (y * g.astype(jnp.float32)).astype(x.dtype)


def _layer_norm(x, g, b):
    x32 = x.astype(jnp.float32)
    mu = jnp.mean(x32, axis=-1, keepdims=True)
    var = jnp.mean(jnp.square(x32 - mu), axis=-1, keepdims=True)
    y = (x32 - mu) * lax.rsqrt(var + EPS)
    return (y * g.astype(jnp.float32) + b.astype(jnp.float32)).astype(x.dtype)


def _modulate(h, shift, scale):
    return h * (1.0 + scale) + shift


def _rotate_half(x):
    x1, x2 = jnp.split(x, 2, axis=-1)
    return jnp.concatenate([-x2, x1], axis=-1)


def _rope(x, cos, sin):
    return x * cos[None, :, None, :] + _rotate_half(x) * sin[None, :, None, :]


def _axial_rope_tables(n_tokens, dtype):
    rows = n_tokens // GRID_W
    t = jnp.arange(rows * GRID_W, dtype=jnp.int32)
    row = (t // GRID_W).astype(jnp.float32)
    col = (t % GRID_W).astype(jnp.float32)
    n_freq = HEAD_DIM // 4
    inv = ROPE_BASE ** (-jnp.arange(n_freq, dtype=jnp.float32) / n_freq)
    theta = jnp.concatenate([row[:, None] * inv, col[:, None] * inv], axis=-1)
    theta = jnp.concatenate([theta, theta], axis=-1)
    return jnp.cos(theta).astype(dtype), jnp.sin(theta).astype(dtype)


def _sink_softmax(scores, sink):
    col = jnp.broadcast_to(sink.astype(jnp.float32).reshape(KV_HEADS, GQ)[:, :, None, None],
                           scores.shape[:-1] + (1,))
    p = jax.nn.softmax(jnp.concatenate([scores, col], axis=-1), axis=-1)
    return p[..., :-1]


def _window_attention(q, k, v, k_ctx, v_ctx, sink):
    B, S = q.shape[0], q.shape[1]
    nb = S // BLOCK
    scale = HEAD_DIM ** -0.5
    qb = q.reshape(B, nb, BLOCK, KV_HEADS, GQ, HEAD_DIM)
    pad = ((0, 0), (BLOCK, BLOCK), (0, 0), (0, 0))
    kp = jnp.pad(k, pad).reshape(B, nb + 2, BLOCK, KV_HEADS, HEAD_DIM)
    vp = jnp.pad(v, pad).reshape(B, nb + 2, BLOCK, KV_HEADS, HEAD_DIM)
    kw = jnp.concatenate([kp[:, :-2], kp[:, 1:-1], kp[:, 2:]], axis=2)
    vw = jnp.concatenate([vp[:, :-2], vp[:, 1:-1], vp[:, 2:]], axis=2)
    s_win = jnp.einsum('bnqkgd,bnjkd->bnkgqj', qb, kw).astype(jnp.float32) * scale
    s_ctx = jnp.einsum('bnqkgd,bckd->bnkgqc', qb, k_ctx).astype(jnp.float32) * scale
    n = jnp.arange(nb)[:, None, None]
    r = jnp.arange(BLOCK)[None, :, None]
    j = jnp.arange(3 * BLOCK)[None, None, :]
    qpos = n * BLOCK + r
    kpos = (n - 1) * BLOCK + j
    mask = (jnp.abs(kpos - qpos) <= WINDOW) & (kpos >= 0) & (kpos < S)
    s_win = jnp.where(mask[None, :, None, None], s_win, -1e30)
    p = _sink_softmax(jnp.concatenate([s_win, s_ctx], axis=-1), sink)
    p_win = p[..., :3 * BLOCK].astype(v.dtype)
    p_ctx = p[..., 3 * BLOCK:].astype(v.dtype)
    out = (jnp.einsum('bnkgqj,bnjkd->bnqkgd', p_win, vw)
           + jnp.einsum('bnkgqc,bckd->bnqkgd', p_ctx, v_ctx))
    return out.reshape(B, S, ATTN_W)


def _context_attention(q, k, v, sink):
    B, C = q.shape[0], q.shape[1]
    qg = q.reshape(B, C, KV_HEADS, GQ, HEAD_DIM)
    s = jnp.einsum('bqkgd,bjkd->bkgqj', qg, k).astype(jnp.float32) * (HEAD_DIM ** -0.5)
    p = _sink_softmax(s, sink).astype(v.dtype)
    return jnp.einsum('bkgqj,bjkd->bqkgd', p, v).reshape(B, C, ATTN_W)


def _conv_module(u, w, b, ln_g, ln_b):
    a, gt = jnp.split(u, 2, axis=-1)
    h = a * jax.nn.sigmoid(gt)
    h = lax.conv_general_dilated(h, w[:, None, :], window_strides=(1,),
                                 padding=[(CONV_K // 2, CONV_K // 2)],
                                 dimension_numbers=('NWC', 'WIO', 'NWC'),
                                 feature_group_count=CONV_CH) + b
    return jax.nn.silu(_layer_norm(h, ln_g, ln_b))


def _mlp(h, w1, w2):
    return jnp.square(jax.nn.relu(h @ w1)) @ w2


def setup_inputs(seed: int = 0) -> dict:
    key = jax.random.key(seed)
    ks = jax.random.split(key, 20)
    f32 = jnp.float32
    nrm = lambda k, shape, s: jax.random.normal(k, shape, f32) * s
    return {
        'x': nrm(ks[0], (BATCH, SEQ, D_MODEL), 1.0),
        'c': nrm(ks[1], (BATCH, D_MODEL), 1.0),
        'ctx': nrm(ks[2], (BATCH, CTX_LEN, D_MODEL), 1.0),
        'c_ctx': nrm(ks[3], (D_MODEL,), 1.0),
        'w_ada': nrm(ks[4], (DEPTH, D_MODEL, 6 * D_MODEL), 0.5 * D_MODEL ** -0.5),
        'b_ada': nrm(ks[5], (DEPTH, 6 * D_MODEL), 0.02),
        'g_mix': 1.0 + nrm(ks[6], (DEPTH, D_MODEL), 0.1),
        'g_mlp': 1.0 + nrm(ks[7], (DEPTH, D_MODEL), 0.1),
        'w_in': nrm(ks[8], (DEPTH, D_MODEL, IN_W), D_MODEL ** -0.5),
        'attn_sink': nrm(ks[9], (DEPTH, ATTN_HEADS), 1.0),
        'conv_w': nrm(ks[10], (DEPTH, CONV_K, CONV_CH), CONV_K ** -0.5),
        'conv_b': nrm(ks[11], (DEPTH, CONV_CH), 0.02),
        'conv_ln_g': 1.0 + nrm(ks[12], (DEPTH, CONV_CH), 0.1),
        'conv_ln_b': nrm(ks[13], (DEPTH, CONV_CH), 0.02),
        'w_out': nrm(ks[14], (DEPTH, MIX_W, D_MODEL), MIX_W ** -0.5),
        'w_mlp1': nrm(ks[15], (DEPTH, D_MODEL, D_FF), D_MODEL ** -0.5),
        'w_mlp2': nrm(ks[16], (DEPTH, D_FF, D_MODEL), D_FF ** -0.5),
        'g_final': 1.0 + nrm(ks[17], (D_MODEL,), 0.1),
    }


def reference(x, c, ctx, c_ctx, w_ada, b_ada, g_mix, g_mlp, w_in, attn_sink, conv_w, conv_b,
              conv_ln_g, conv_ln_b, w_out, w_mlp1, w_mlp2, g_final):
    B, S = x.shape[0], x.shape[1]
    C = ctx.shape[1]
    cos, sin = _axial_rope_tables(S, x.dtype)
    silu_c = jax.nn.silu(c)
    silu_cc = jax.nn.silu(c_ctx)
    split_pts = [ATTN_W, ATTN_W + KV_W, ATTN_W + 2 * KV_W]
    for l in range(DEPTH):
        last = l == DEPTH - 1
        sh1, sc1, g1, sh2, sc2, g2 = jnp.split((silu_c @ w_ada[l] + b_ada[l])[:, None, :], 6, axis=-1)
        csh1, csc1, cg1, csh2, csc2, cg2 = jnp.split(silu_cc @ w_ada[l] + b_ada[l], 6, axis=-1)

        h = _modulate(_rmsnorm(x, g_mix[l]), sh1, sc1)
        hc = _modulate(_rmsnorm(ctx, g_mix[l]), csh1, csc1)
        q, k, v, u = jnp.split(h @ w_in[l], split_pts, axis=-1)
        q = _rope(q.reshape(B, S, ATTN_HEADS, HEAD_DIM), cos, sin)
        k = _rope(k.reshape(B, S, KV_HEADS, HEAD_DIM), cos, sin)
        v = v.reshape(B, S, KV_HEADS, HEAD_DIM)
        if last:
            kc, vc = jnp.split(hc @ w_in[l][:, ATTN_W:ATTN_W + 2 * KV_W], 2, axis=-1)
        else:
            qc, kc, vc, uc = jnp.split(hc @ w_in[l], split_pts, axis=-1)
        kc = kc.reshape(B, C, KV_HEADS, HEAD_DIM)
        vc = vc.reshape(B, C, KV_HEADS, HEAD_DIM)

        att = _window_attention(q, k, v, kc, vc, attn_sink[l])
        cv = _conv_module(u, conv_w[l], conv_b[l], conv_ln_g[l], conv_ln_b[l])
        x = x + g1 * (jnp.concatenate([att, cv], axis=-1) @ w_out[l])
        if not last:
            att_c = _context_attention(qc.reshape(B, C, ATTN_HEADS, HEAD_DIM), kc, vc, attn_sink[l])
            cv_c = _conv_module(uc, conv_w[l], conv_b[l], conv_ln_g[l], conv_ln_b[l])
            ctx = ctx + cg1 * (jnp.concatenate([att_c, cv_c], axis=-1) @ w_out[l])

        x = x + g2 * _mlp(_modulate(_rmsnorm(x, g_mlp[l]), sh2, sc2), w_mlp1[l], w_mlp2[l])
        if not last:
            ctx = ctx + cg2 * _mlp(_modulate(_rmsnorm(ctx, g_mlp[l]), csh2, csc2), w_mlp1[l], w_mlp2[l])
    return _rmsnorm(x, g_final)
```

```python
import contextlib
import numpy as np
import concourse.bass as bass
import concourse.mybir as mybir
from concourse.bass_utils import run_bass_kernel_spmd

F32 = mybir.dt.float32
BF16 = mybir.dt.bfloat16
AF = mybir.ActivationFunctionType
ALU = mybir.AluOpType

D = 2048
NLAT = 2560
NCTX = 256
NT = NLAT + NCTX
NOWN = 2048
HCW = 2880
GROUPS = [(0, 512), (512, 512), (1024, 512), (1536, 512), (2048, 512), (2560, 256)]
SGS = [(0, 1024, [0, 1]), (1024, 1024, [2, 3]), (2048, 768, [4, 5])]
EPS = 1e-6
SCALE = 128.0 ** -0.5


class Buf:
    __slots__ = ("name", "w", "r")

    def __init__(self, name):
        self.name = name
        self.w = None
        self.r = {}


class Rec:
    def __init__(self):
        self.calls = []

    def __getattr__(self, name):
        def f(*a, **k):
            self.calls.append((name, a, k))
            return self
        return f


class Prog:
    ENGS = ("pe", "act", "dve", "pool", "sp")

    def __init__(self, nc, stack):
        self.nc = nc
        self.stack = stack
        self.ops = {e: [] for e in self.ENGS}
        self.cnt = {}
        self.sems = {}
        self.seen = {e: {} for e in self.ENGS}
        self.bufs = {}
        for e in self.ENGS:
            self._sem(e)

    def B(self, *key):
        b = self.bufs.get(key)
        if b is None:
            b = self.bufs[key] = Buf(key)
        return b

    def _sem(self, key):
        if key not in self.sems:
            self.sems[key] = self.stack.enter_context(self.nc.semaphore("s_" + key))
            self.cnt[key] = 0
        return self.sems[key]

    def op(self, eng, fn, reads=(), writes=(), dma=None):
        deps = {}
        for b in reads:
            if b.w is not None and deps.get(b.w[0], 0) < b.w[1]:
                deps[b.w[0]] = b.w[1]
            if b.name[0] == "ps":
                for k, v in b.r.items():
                    if k != eng and deps.get(k, 0) < v:
                        deps[k] = v
        for b in writes:
            if b.w is not None and deps.get(b.w[0], 0) < b.w[1]:
                deps[b.w[0]] = b.w[1]
            for k, v in b.r.items():
                if deps.get(k, 0) < v:
                    deps[k] = v
        if dma is None:
            key = eng
            inc = 1
        else:
            key = "d_" + dma
            self._sem(key)
            inc = 16
        self.cnt[key] += inc
        tick = (key, self.cnt[key])
        waits = []
        for k, v in deps.items():
            if k == eng:
                if eng == "pe":
                    continue
                if dma is None and self.cnt[eng] - v > 3:
                    continue
            if self.seen[eng].get(k, 0) >= v:
                continue
            self.seen[eng][k] = v
            waits.append((k, v))
        for b in reads:
            if b.r.get(tick[0], 0) < tick[1]:
                b.r[tick[0]] = tick[1]
        for b in writes:
            b.w = tick
            b.r = {}
        rec = Rec()
        fn(rec)
        if dma is not None and len(rec.calls) > 1:
            extra = 16 * (len(rec.calls) - 1)
            self.cnt[key] += extra
            tick = (key, self.cnt[key])
            for b in reads:
                b.r[tick[0]] = tick[1]
            for b in writes:
                b.w = tick
        self.ops[eng].append((waits, rec.calls, key, inc))

    def wait_all(self, eng, bufs):
        waits = []
        for b in bufs:
            items = list(b.r.items())
            if b.w is not None:
                items.append(b.w)
            for k, v in items:
                if self.seen[eng].get(k, 0) >= v:
                    continue
                self.seen[eng][k] = v
                waits.append((k, v))
        self.ops[eng].append((waits, None, None, 0))

    def barrier(self):
        snap = {k: v for k, v in self.cnt.items() if v > 0 and k != "sp"}
        for e in self.ENGS:
            waits = []
            for k, v in snap.items():
                if k == e:
                    continue
                if self.seen[e].get(k, 0) >= v:
                    continue
                self.seen[e][k] = v
                waits.append((k, v))
            if waits:
                self.ops[e].append((waits, None, None, 0))

    def emit(self):
        nc = self.nc
        with nc.Block() as block:
            def run(name):
                def body(e):
                    for waits, fn, key, inc in self.ops[name]:
                        for k, v in waits:
                            e.wait_ge(self.sems[k], v)
                        if fn is not None:
                            for name_, a, k in fn:
                                ins = getattr(e, name_)(*a, **k)
                                if inc == 16:
                                    ins.then_inc(self.sems[key], 16)
                            if inc != 16:
                                ins.then_inc(self.sems[key], inc)
                return body
            block.tensor(run("pe"))
            block.scalar(run("act"))
            block.vector(run("dve"))
            block.gpsimd(run("pool"))
            block.sync(run("sp"))


class Carver:
    def __init__(self, arena, base, limit):
        self.arena = arena
        self.off = base
        self.limit = limit

    def take(self, shape, dt):
        n = 1
        for s in shape[1:]:
            n *= s
        nb = n * (2 if dt == BF16 else 4)
        nb = (nb + 31) // 32 * 32
        assert self.off + nb <= self.limit, (self.off, nb, self.limit)
        v = self.arena[:, self.off // 2:(self.off + nb) // 2]
        if dt == F32:
            v = v.bitcast(F32)
        v = v[:, 0:n]
        if len(shape) == 3:
            v = v.rearrange("p (a b) -> p a b", a=shape[1])
        self.off += nb
        return v


ARENA_BYTES = 194816
S0 = 112640


def build_program(L):
    nc = bass.Bass("TRN2", target_bir_lowering=False)

    def din(name, shape):
        return nc.dram_tensor(name, shape, F32, kind="ExternalInput").ap()

    xin = din("xin", [16, 128, NT])
    cT_d = din("cT", [128, 16, 2])
    rope_d = din("rope", [128, 2, NT])
    cst_d = din("cst", [128, 128 + 1024])
    w_ada = din("w_ada", [L, D, 6 * D])
    bada_d = din("b_ada2", [128, L * 96 * 2])
    gvec_d = din("gvec", [128, (2 * L + 1) * 16])
    w_in = din("w_in", [L, D, 3584])
    sink_d = din("sinkrow", [1, L * 2 * 512])
    convp_d = din("convp", [128, L * 8 * 34])
    convw_d = din("convw", [128, L * 8192])
    w_out = din("w_out", [L, D, D])
    w1_d = din("w1", [L, D, 4 * D])
    w2_d = din("w2", [L, 4 * D, D])
    y_d = nc.dram_tensor("y", [16, 128, NOWN], F32, kind="ExternalOutput").ap()
    xs = nc.dram_tensor("xs", [16, 128, NT], F32).ap()
    qs = nc.dram_tensor("qs", [8, 128, NT], BF16).ap()
    hcs = nc.dram_tensor("hcs", [8, 128, HCW], BF16).ap()
    dgs = nc.dram_tensor("dgs", [L * 8, 128, 31 * 128], BF16).ap()

    with contextlib.ExitStack() as st:
        P = Prog(nc, st)
        B = P.B

        def sb(name, shape, dt):
            return st.enter_context(nc.sbuf_tensor(name, shape, dt))

        arena = sb("arena", [128, ARENA_BYTES // 2], BF16)
        ones_f = sb("ones_f", [128, 128], F32)
        ones_b = sb("ones_b", [128, 128], BF16)
        ident_b = sb("ident_b", [128, 128], BF16)
        perm_b = sb("perm_b", [128, 128], BF16)
        mask_b = sb("mask_b", [128, 2, 512], BF16)
        eps_t = sb("eps_t", [128, 1], F32)
        silu_c = sb("silu_c", [128, 16, 2], BF16)
        mods = [sb(f"mod{i}", [128, 96, 2], F32) for i in range(2)]
        gmods = [sb(f"gmod{i}", [128, 2, 16, 2], F32) for i in range(2)]
        badat = sb("badat", [128, L * 96, 2], F32)
        gvec = sb("gvec_t", [128, 2 * L + 1, 16], F32)
        esink = sb("esink", [1, 2 * 512], BF16)
        convp = sb("convp_t", [128, L * 8, 34], F32)
        ps = [st.enter_context(nc.psum_tensor(f"ps{i}", [128, 512], F32)) for i in range(8)]

        def PS(i):
            return B("ps", i)

        H = Carver(arena, 0, 90112).take([128, 16, NT], BF16)
        KT = Carver(arena, 90112, 101376).take([128, 2, NT], BF16)
        V = Carver(arena, 101376, 112640).take([128, 22, 256], BF16)

        def gof(t):
            return 5 if t >= NLAT else t // 512

        def vmod(g):
            return 1 if g == 5 else 0

        A = Carver(arena, S0, ARENA_BYTES)
        cst_f = A.take([128, 1152], F32)
        cT_f = A.take([128, 32], F32)
        zt = A.take([128, 8, 32], BF16)
        P.op("sp", lambda e: e.dma_start(out=cst_f, in_=cst_d), writes=[B("cst_f")], dma="p0")
        P.op("sp", lambda e: e.dma_start(out=cT_f, in_=cT_d.rearrange("p a b -> p (a b)")), writes=[B("cT_f")], dma="p1")
        P.op("sp", lambda e: e.dma_start(out=badat[:].rearrange("p a b -> p (a b)"), in_=bada_d), writes=[B("bada")], dma="p3")
        P.op("sp", lambda e: e.dma_start(out=gvec[:].rearrange("p a b -> p (a b)"), in_=gvec_d), writes=[B("gvec")], dma="p4")
        P.op("sp", lambda e: e.dma_start(out=convp[:].rearrange("p a b -> p (a b)"), in_=convp_d), writes=[B("convp")], dma="p5")
        for c in range(16):
            P.op("sp", lambda e, c=c: e.dma_start(out=xs[c], in_=xin[c]), writes=[B("xs", c)], dma="p6")
        P.op("dve", lambda e: e.memset(ones_f[:], 1.0), writes=[B("ones")])
        P.op("dve", lambda e: e.memset(ones_b[:], 1.0), writes=[B("ones")])
        P.op("dve", lambda e: e.memset(eps_t[:], EPS), writes=[B("ones")])
        P.op("dve", lambda e: e.memset(zt, 0.0), writes=[B("zt")])
        P.op("dve", lambda e: e.tensor_copy(out=ident_b[:], in_=cst_f[:, 0:128]), reads=[B("cst_f")], writes=[B("ones")])
        P.op("dve", lambda e: e.tensor_copy(out=perm_b[:, 0:64], in_=cst_f[:, 64:128]), reads=[B("cst_f")], writes=[B("ones")])
        P.op("dve", lambda e: e.tensor_copy(out=perm_b[:, 64:128], in_=cst_f[:, 0:64]), reads=[B("cst_f")], writes=[B("ones")])
        P.op("dve", lambda e: e.tensor_copy(out=mask_b[:].rearrange("p a b -> p (a b)"), in_=cst_f[:, 128:1152]),
             reads=[B("cst_f")], writes=[B("ones")])
        P.op("act", lambda e: e.activation(out=silu_c[:].rearrange("p a b -> p (a b)"), in_=cT_f, func=AF.Silu),
             reads=[B("cT_f")], writes=[B("siluc")])
        hv = hcs.rearrange("c p t -> p c t")
        P.op("sp", lambda e: e.dma_start(out=hv[:, :, 0:16], in_=zt[:, :, 0:16]), reads=[B("zt")], writes=[B("hcs_pad")], dma="p7")
        P.op("sp", lambda e: e.dma_start(out=hv[:, :, 2576:2608], in_=zt[:, :, 0:32]), reads=[B("zt")], writes=[B("hcs_pad")], dma="p7")
        P.op("sp", lambda e: e.dma_start(out=hv[:, :, 2864:2880], in_=zt[:, :, 0:16]), reads=[B("zt")], writes=[B("hcs_pad")], dma="p7")
        P.barrier()

        def wview(w2d):
            return w2d.rearrange("(kc p) c -> p kc c", p=128)

        def ada_mm(lx, oc, wt, col, wbuf, bank):
            def mm(e):
                for kc in range(16):
                    ins = e.matmul(ps[bank][:, 2 * (oc % 4):2 * (oc % 4) + 2], lhsT=wt[:, kc, col:col + 128],
                                   rhs=silu_c[:, kc, :], start=(kc == 0), stop=(kc == 15))
                return ins
            P.op("pe", mm, reads=[wbuf, B("siluc")], writes=[PS(bank)])

        def ada_evac(lx, s4, bank):
            P.op("dve", lambda e: e.tensor_tensor(
                out=mods[lx % 2][:, 4 * s4:4 * s4 + 4, :], in0=ps[bank][:, 0:8].rearrange("p (j v) -> p j v", v=2),
                in1=badat[:, lx * 96 + 4 * s4:lx * 96 + 4 * s4 + 4, :], op=ALU.add),
                reads=[PS(bank), B("bada")], writes=[B("mod", lx % 2)])

        def ada_gmod(lx, parts=(0, 1)):
            m_, g_ = mods[lx % 2], gmods[lx % 2]
            for v in range(2):
                if 0 in parts:
                    P.op("dve", lambda e: e.scalar_tensor_tensor(out=g_[:, 0, :, v], in0=m_[:, 16:32, v], scalar=1.0,
                                                                 in1=gvec[:, lx, :], op0=ALU.add, op1=ALU.mult),
                         reads=[B("mod", lx % 2), B("gvec")], writes=[B("gmod", lx % 2)])
                if 1 in parts:
                    P.op("dve", lambda e: e.scalar_tensor_tensor(out=g_[:, 1, :, v], in0=m_[:, 64:80, v], scalar=1.0,
                                                                 in1=gvec[:, L + lx, :], op0=ALU.add, op1=ALU.mult),
                         reads=[B("mod", lx % 2), B("gvec")], writes=[B("gmod", lx % 2)])

        feed = {"oc": 96, "end": 96}

        def feed_load(oc):
            i = oc % 2
            P.op("pool", lambda e: e.dma_start(out=feed["wa"][i], in_=feed["wav"][:, :, oc * 128:(oc + 1) * 128]),
                 writes=[B("wa", i)], dma=f"wa{i}")

        def feed_begin(lx, oc0, oc1, wa_tiles, bank):
            feed.update(lx=lx, oc=oc0, end=oc1, wa=wa_tiles, bank=bank, wav=wview(w_ada[lx]))
            feed_load(oc0)

        def feed_step():
            if feed["oc"] >= feed["end"]:
                return
            oc = feed["oc"]
            feed["oc"] += 1
            if oc + 1 < feed["end"]:
                feed_load(oc + 1)
            ada_mm(feed["lx"], oc, feed["wa"][oc % 2], 0, B("wa", oc % 2), feed["bank"])
            if oc % 4 == 3:
                ada_evac(feed["lx"], oc // 4, feed["bank"])

        def feed_finish():
            while feed["oc"] < feed["end"]:
                feed_step()

        def do_esink(lx, sink_f):
            P.op("sp", lambda e: e.dma_start(out=sink_f[0:1, :], in_=sink_d[:, lx * 1024:(lx + 1) * 1024]), writes=[B("sink_f")], dma="p2")
            P.op("act", lambda e: e.activation(out=esink[:], in_=sink_f[0:1, :], func=AF.Exp), reads=[B("sink_f")], writes=[B("esink")])

        for l in range(L):
            last = l == L - 1
            rr = L - 1 - l
            nAB = 128 * (rr + 1)
            nCD = 128 * rr
            GAB = [(g, GROUPS[g][0], GROUPS[g][1]) for g in range(4)] + [(4, 2048, nAB), (5, 2560, 256)]
            GCD = [(g, GROUPS[g][0], GROUPS[g][1]) for g in range(4)]
            if nCD > 0:
                GCD.append((4, 2048, nCD))
            if not last:
                GCD.append((5, 2560, 256))
            kb_max = 15 + nAB // 128
            mod = mods[l % 2]
            gmod = gmods[l % 2]
            bmod = B("mod", l % 2)
            bgmod = B("gmod", l % 2)
            if l == 0:
                A = Carver(arena, S0, ARENA_BYTES)
                wsl = [A.take([128, 16, 512], BF16) for _ in range(3)]
                sink_f = A.take([128, 1024], F32)
                dgt = [A.take([128, 31 * 128], BF16) for _ in range(2)]
                do_esink(0, sink_f)
                wv = wview(w_ada[0])

                def ada_load(s_):
                    i = s_ % 3
                    P.op("pool", lambda e: e.dma_start(out=wsl[i], in_=wv[:, :, s_ * 512:(s_ + 1) * 512]),
                         writes=[B("ws", i)], dma=f"ws{i}")

                def mkdiag_set(lc):
                    di = lc % 2

                    def mkdiag(e):
                        for k in range(31):
                            ins = e.tensor_scalar_mul(out=dgt[di][:, k * 128:(k + 1) * 128], in0=ident_b[:], scalar1=convp[:, lc, k:k + 1])
                        return ins
                    P.op("dve", mkdiag, reads=[B("ones"), B("convp")], writes=[B("dgt", di)])
                    P.op("sp", lambda e: e.dma_start(out=dgs[lc], in_=dgt[di]), reads=[B("dgt", di)], writes=[B("dgs")], dma=f"dgt{di}")

                NSL = 8
                ada_load(0)
                ada_load(1)
                nsets = 0
                for s_ in range(NSL):
                    if s_ + 2 < NSL:
                        ada_load(s_ + 2)
                    i = s_ % 3
                    bank = s_ % 2
                    for j in range(4):
                        ada_mm(0, 4 * s_ + j, wsl[i], j * 128, B("ws", i), bank)
                    ada_evac(0, s_, bank)
                ada_gmod(0, parts=(0,))
                P.barrier()

            A = Carver(arena, S0, ARENA_BYTES)
            xg = [A.take([128, 16, 512], F32) for _ in range(2)]
            sqb = [A.take([128, 512], BF16) for _ in range(4)]
            tmp = [A.take([128, 512], F32) for _ in range(2)]
            rstd = A.take([128, 512], F32)
            if l > 0:
                do_esink(l, A.take([128, 1024], F32))
            ctr = [0, 0]
            for g, t0, n in GAB:
                v = vmod(g)
                gi_ = g % 2
                P.op("sp", lambda e: e.dma_start(out=xg[gi_][:, :, :n], in_=xs[:, :, t0:t0 + n].rearrange("c p t -> p c t")),
                     reads=[B("xs", c) for c in range(16)], writes=[B("xg", gi_)], dma=f"xg{gi_}")
                for c in range(16):
                    i2 = ctr[0] % 4
                    ctr[0] += 1
                    P.op("dve" if c % 4 == 3 else "pool", lambda e: e.tensor_tensor(out=sqb[i2][:, :n], in0=xg[gi_][:, c, :n], in1=xg[gi_][:, c, :n], op=ALU.mult),
                         reads=[B("xg", gi_)], writes=[B("sq", i2)])
                    P.op("pe", lambda e: e.matmul(ps[7][:, :n], lhsT=ones_b[:], rhs=sqb[i2][:, :n], start=(c == 0), stop=(c == 15)),
                         reads=[B("sq", i2), B("ones")], writes=[PS(7)])
                P.op("act", lambda e: e.activation(out=rstd[:, :n], in_=ps[7][:, :n], func=AF.Sqrt, scale=1.0 / D, bias=eps_t[:]),
                     reads=[PS(7)], writes=[B("rstd")])
                P.op("dve", lambda e: e.reciprocal(out=rstd[:, :n], in_=rstd[:, :n]), reads=[B("rstd")], writes=[B("rstd")])
                for c in range(16):
                    i2 = ctr[1] % 2
                    ctr[1] += 1
                    P.op("dve", lambda e: e.tensor_tensor(out=tmp[i2][:, :n], in0=xg[gi_][:, c, :n], in1=rstd[:, :n], op=ALU.mult),
                         reads=[B("xg", gi_), B("rstd")], writes=[B("tmp", i2)])
                    P.op("act", lambda e: e.activation(out=H[:, c, t0:t0 + n], in_=tmp[i2][:, :n], func=AF.Identity,
                                                       scale=gmod[:, 0, c, v:v + 1], bias=mod[:, c, v:v + 1]),
                         reads=[B("tmp", i2), bgmod, bmod], writes=[B("BIG", g)])
            P.barrier()

            A = Carver(arena, S0, ARENA_BYTES)
            wsl = [A.take([128, 16, 512], BF16) for _ in range(4)]
            rpc = Carver(arena, S0 + 16384, S0 + 32768)
            rp = [rpc.take([128, 2, 512], F32) for _ in range(4)]
            rctr = [0]
            t1 = [A.take([128, 512], F32) for _ in range(2)]
            t2 = [A.take([128, 512], F32) for _ in range(2)]
            ot = [A.take([128, 512], BF16) for _ in range(2)]
            qtmp = [A.take([128, 512], BF16) for _ in range(2)]
            wi = wview(w_in[l])
            steps = [("q", 0), ("q", 1), ("kv", 0), ("u", 0), ("u", 1)]

            def b_load(t):
                kind, i = steps[t]
                sa, sbb = 2 * (t % 2), 2 * (t % 2) + 1
                if kind == "q":
                    srcA, srcB = wi[:, :, i * 512:(i + 1) * 512], None
                elif kind == "kv":
                    srcA, srcB = wi[:, :, 1024:1536], None
                else:
                    srcA, srcB = wi[:, :, 1536 + i * 512:2048 + i * 512], wi[:, :, 2560 + i * 512:3072 + i * 512]
                P.op("pool", lambda e: e.dma_start(out=wsl[sa], in_=srcA), writes=[B("ws", sa)], dma=f"ws{sa}")
                if srcB is not None:
                    P.op("pool", lambda e: e.dma_start(out=wsl[sbb], in_=srcB),
                         writes=[B("ws", sbb)] + ([B("rp", k) for k in range(4)] if sbb == 1 else []), dma=f"ws{sbb}")

            bctr = [0, 0]
            b_load(0)
            for t, (kind, i) in enumerate(steps):
                if t + 1 < len(steps):
                    b_load(t + 1)
                sa, sbb = 2 * (t % 2), 2 * (t % 2) + 1
                nj = 2 if kind == "kv" else 4

                def rope_tail(u):
                    j, g, t0, n, pa, pb, r, rq = u
                    P.op("pe", lambda e: e.matmul(ps[pb][:, :n], lhsT=perm_b[:], rhs=qtmp[r][:, :n], start=True, stop=True),
                         reads=[B("qtmp", r), B("ones")], writes=[PS(pb)])
                    P.op("dve", lambda e: e.tensor_tensor(out=t1[r][:, :n], in0=ps[pa][:, :n], in1=rp[rq][:, 0, :n], op=ALU.mult),
                         reads=[PS(pa), B("rp", rq)], writes=[B("t1", r)])
                    P.op("dve", lambda e: e.tensor_tensor(out=t2[r][:, :n], in0=ps[pb][:, :n], in1=rp[rq][:, 1, :n], op=ALU.mult),
                         reads=[PS(pb), B("rp", rq)], writes=[B("t2", r)])
                    if kind == "q":
                        h = 4 * i + j
                        P.op("dve", lambda e: e.tensor_tensor(out=ot[r][:, :n], in0=t1[r][:, :n], in1=t2[r][:, :n], op=ALU.add),
                             reads=[B("t1", r), B("t2", r)], writes=[B("ot", r)])
                        P.op("sp", lambda e: e.dma_start(out=qs[h, :, t0:t0 + n], in_=ot[r][:, :n]),
                             reads=[B("ot", r)], writes=[B("qs", g)], dma=f"ot{r}")
                    else:
                        P.op("dve", lambda e: e.tensor_tensor(out=KT[:, j, t0:t0 + n], in0=t1[r][:, :n], in1=t2[r][:, :n], op=ALU.add),
                             reads=[B("t1", r), B("t2", r)], writes=[B("KT", g)])

                pending = None
                for j in range(nj):
                    for g, t0, n in GAB:
                        pa = 2 * (bctr[0] % 3)
                        pb = pa + 1
                        bctr[0] += 1
                        r = bctr[1] % 2
                        bctr[1] += 1

                        def mmA(e):
                            for kc in range(16):
                                ins = e.matmul(ps[pa][:, :n], lhsT=wsl[sa][:, kc, j * 128:(j + 1) * 128], rhs=H[:, kc, t0:t0 + n],
                                               start=(kc == 0), stop=(kc == 15))
                            return ins
                        P.op("pe", mmA, reads=[B("ws", sa), B("BIG", g)], writes=[PS(pa)])
                        if kind in ("q", "kv"):
                            rq = rctr[0] % 4
                            rctr[0] += 1
                            P.op("sp", lambda e: e.dma_start(out=rp[rq][:, :, :n], in_=rope_d[:, :, t0:t0 + n]),
                                 reads=[B("ws", 1)], writes=[B("rp", rq)], dma=f"rp{rq}")
                            P.op("act", lambda e: e.activation(out=qtmp[r][:, :n], in_=ps[pa][:, :n], func=AF.Copy),
                                 reads=[PS(pa)], writes=[B("qtmp", r)])
                            if pending is not None:
                                rope_tail(pending)
                            pending = (j, g, t0, n, pa, pb, r, rq)
                        else:
                            def mmB(e):
                                for kc in range(16):
                                    ins = e.matmul(ps[pb][:, :n], lhsT=wsl[sbb][:, kc, j * 128:(j + 1) * 128], rhs=H[:, kc, t0:t0 + n],
                                                   start=(kc == 0), stop=(kc == 15))
                                return ins
                            P.op("pe", mmB, reads=[B("ws", sbb), B("BIG", g)], writes=[PS(pb)])
                            c = 4 * i + j
                            col0 = 2608 if g == 5 else 16 + t0
                            P.op("act", lambda e: e.activation(out=t2[r][:, :n], in_=ps[pb][:, :n], func=AF.Sigmoid),
                                 reads=[PS(pb)], writes=[B("t2", r)])
                            P.op("dve", lambda e: e.tensor_tensor(out=ot[r][:, :n], in0=ps[pa][:, :n], in1=t2[r][:, :n], op=ALU.mult),
                                 reads=[PS(pa), B("t2", r)], writes=[B("ot", r)])
                            P.op("sp", lambda e: e.dma_start(out=hcs[c, :, col0:col0 + n], in_=ot[r][:, :n]),
                                 reads=[B("ot", r)], writes=[B("hcs", g)], dma=f"ot{r}")
                if pending is not None:
                    rope_tail(pending)
                if kind == "kv":
                    for bix, blk in enumerate(list(range(16 + nAB // 128)) + [20, 21]):
                        pv_ = 6 + bix % 2

                        def mmV(e, blk=blk, pv_=pv_, sa=sa):
                            for kc in range(16):
                                ins = e.matmul(ps[pv_][:, 0:256], lhsT=H[:, kc, blk * 128:(blk + 1) * 128], rhs=wsl[sa][:, kc, 256:512],
                                               start=(kc == 0), stop=(kc == 15))
                            return ins
                        P.op("pe", mmV, reads=[B("ws", sa), B("BIG", gof(blk * 128))], writes=[PS(pv_)])
                        P.op("act", lambda e, blk=blk, pv_=pv_: e.activation(out=V[:, blk, :], in_=ps[pv_][:, 0:256], func=AF.Copy),
                             reads=[PS(pv_)], writes=[B("V", blk)])
            P.barrier()

            A = Carver(arena, S0, ARENA_BYTES)
            qg = [A.take([128, 8, 512], BF16) for _ in range(2)]
            pt = [A.take([128, 5, 512], BF16) for _ in range(3)]
            rc = [A.take([128, 512], F32) for _ in range(2)]
            if l == 0:
                feed_begin(0, 32, 52, [A.take([128, 16, 128], BF16) for _ in range(2)], 7)
            cctr = [0, 0, 0]
            pending_c = None

            def attn_tail(g, t0, bi, kvh, keys, pi):
                pvb = 3 + cctr[2] % 2
                dnb = 5 + cctr[2] % 2
                ri = cctr[2] % 2
                cctr[2] += 1
                nk = len(keys)

                def mmPV(e):
                    for idx, (kb, m) in enumerate(keys):
                        ins = e.matmul(ps[pvb][:], lhsT=V[:, kb, kvh * 128:(kvh + 1) * 128], rhs=pt[pi][:, idx, :],
                                       start=(idx == 0), stop=(idx == nk - 1))
                    return ins

                def mmDN(e):
                    for idx in range(nk):
                        e.matmul(ps[dnb][:], lhsT=ones_b[:], rhs=pt[pi][:, idx, :], start=(idx == 0), stop=False)
                    o = kvh * 512
                    return e.matmul(ps[dnb][:], lhsT=ones_b[0:1, :], rhs=esink[0:1, o:o + 512], start=False, stop=True)
                P.op("pe", mmPV, reads=[B("pt", pi)] + [B("V", kb) for kb, _ in keys], writes=[PS(pvb)])
                P.op("pe", mmDN, reads=[B("pt", pi), B("ones"), B("esink")], writes=[PS(dnb)])
                P.op("dve", lambda e: e.reciprocal(out=rc[ri][:], in_=ps[dnb][:]), reads=[PS(dnb)], writes=[B("rc", ri)])
                tb = t0 + bi * 128
                P.op("dve", lambda e: e.tensor_tensor(
                    out=H[:, 4 * kvh:4 * kvh + 4, tb:tb + 128], in0=ps[pvb][:].rearrange("p (h r) -> p h r", h=4),
                    in1=rc[ri][:].rearrange("p (h r) -> p h r", h=4), op=ALU.mult),
                    reads=[PS(pvb), B("rc", ri)], writes=[B("BIG", g)])

            for gi, (g, t0, n) in enumerate(GCD):
                qi = gi % 2
                P.op("sp", lambda e: e.dma_start(out=qg[qi][:, :, :n], in_=qs[:, :, t0:t0 + n].rearrange("h p t -> p h t")),
                     reads=[B("qs", g)], writes=[B("qg", qi)], dma=f"qg{qi}")
                for bi in range(n // 128):
                    blk = t0 // 128 + bi
                    for kvh in range(2):
                        if g == 5:
                            keys = [(20, None), (21, None)]
                        else:
                            keys = []
                            if blk - 1 >= 0:
                                keys.append((blk - 1, 0))
                            keys.append((blk, None))
                            if blk + 1 <= kb_max:
                                keys.append((blk + 1, 1))
                            keys += [(20, None), (21, None)]
                        pi = cctr[0] % 3
                        cctr[0] += 1
                        for idx, (kb, m) in enumerate(keys):
                            sbk = cctr[1] % 3
                            cctr[1] += 1

                            def mmS(e):
                                ins = e.matmul(ps[sbk][:].rearrange("p (h r) -> p h r", h=4), lhsT=KT[:, kvh, kb * 128:(kb + 1) * 128],
                                               rhs=qg[qi][:, 4 * kvh:4 * kvh + 4, bi * 128:(bi + 1) * 128], start=True, stop=(m is None))
                                if m is not None:
                                    ins = e.matmul(ps[sbk][:], lhsT=ident_b[:], rhs=mask_b[:, m, :], start=False, stop=True)
                                return ins
                            P.op("pe", mmS, reads=[B("KT", gof(kb * 128)), B("qg", qi), B("ones")], writes=[PS(sbk)])
                            P.op("act", lambda e: e.activation(out=pt[pi][:, idx, :], in_=ps[sbk][:], func=AF.Exp, scale=SCALE),
                                 reads=[PS(sbk)], writes=[B("pt", pi)])
                        if pending_c is not None:
                            attn_tail(*pending_c)
                        pending_c = (g, t0, bi, kvh, keys, pi)
                        if l == 0:
                            feed_step()
            if pending_c is not None:
                attn_tail(*pending_c)
            if l == 0:
                feed_finish()
            P.barrier()

            A = Carver(arena, S0, ARENA_BYTES)
            RT = [A.take([128, 4, 544], BF16) for _ in range(3)]
            accs = [A.take([128, 8, 512], F32) for _ in range(2)]
            cw = A.take([128, 8192], BF16)
            for q4 in range(4):
                P.op("pool", lambda e: e.dma_start(out=cw[:, q4 * 2048:(q4 + 1) * 2048],
                                                   in_=convw_d[:, l * 8192 + q4 * 2048:l * 8192 + (q4 + 1) * 2048]),
                     writes=[B("cw", q4)], dma=f"ws{q4}")
            csq = [A.take([128, 512], BF16) for _ in range(2)]
            accb = [A.take([128, 512], BF16) for _ in range(2)]
            mean = A.take([128, 512], F32)
            crs = A.take([128, 512], F32)
            cctr = [0, 0, 0]

            pend_stats = []
            for gi, (g, t0, n) in enumerate(GCD):
                col0 = 2608 if g == 5 else 16 + t0
                acc = accs[gi % 2]
                sb_, qb_ = (3, 4) if gi % 2 == 0 else (5, 6)
                for c in range(8):
                    cp = convp[:, l * 8 + c, :]
                    di = cctr[0] % 3
                    cctr[0] += 1
                    cb = cctr[1] % 3
                    cctr[1] += 1

                    def ldR(e):
                        src = hcs[c].rearrange("(cbi cc) t -> cc cbi t", cc=32)
                        for j in range(4):
                            ins = e.dma_start(out=RT[di][j * 32:(j + 1) * 32, :, 0:n + 28],
                                              in_=src[:, :, col0 - 15 + j:col0 - 15 + j + n + 28])
                        return ins
                    P.op("sp", ldR, reads=[B("hcs", gg) for gg in range(6)] + [B("hcs_pad")], writes=[B("RT", di)], dma=f"dg{di}")

                    def mmC(e):
                        for q in range(8):
                            for cbi in range(4):
                                wo_ = ((c * 8 + q) * 4 + cbi) * 32
                                ins = e.matmul(ps[cb][cbi * 32:(cbi + 1) * 32, :n], lhsT=cw[:, wo_:wo_ + 32], rhs=RT[di][:, cbi, 4 * q:4 * q + n],
                                               start=(q == 0), stop=(q == 7), tile_position=(0, cbi * 32))
                        return ins
                    P.op("pe", mmC, reads=[B("RT", di)] + [B("cw", q4) for q4 in range(4)], writes=[PS(cb)])
                    P.op("act", lambda e: e.activation(out=acc[:, c, :n], in_=ps[cb][:, :n], func=AF.Identity, bias=cp[:, 31:32]),
                         reads=[PS(cb), B("convp")], writes=[B("acc", gi % 2, c)])
                    i2 = cctr[2] % 2
                    cctr[2] += 1
                    P.op("act", lambda e: e.activation(out=csq[i2][:, :n], in_=acc[:, c, :n], func=AF.Square),
                         reads=[B("acc", gi % 2, c)], writes=[B("csq", i2)])
                    P.op("dve", lambda e: e.tensor_copy(out=accb[i2][:, :n], in_=acc[:, c, :n]),
                         reads=[B("acc", gi % 2, c)], writes=[B("accb", i2)])
                    def stats_mm(c=c, i2=i2, n=n, sb_=sb_, qb_=qb_):
                        P.op("pe", lambda e: e.matmul(ps[sb_][:, :n], lhsT=ones_b[:], rhs=accb[i2][:, :n], start=(c == 0), stop=(c == 7)),
                             reads=[B("accb", i2), B("ones")], writes=[PS(sb_)])
                        P.op("pe", lambda e: e.matmul(ps[qb_][:, :n], lhsT=ones_b[:], rhs=csq[i2][:, :n], start=(c == 0), stop=(c == 7)),
                             reads=[B("csq", i2), B("ones")], writes=[PS(qb_)])
                    pend_stats.append(stats_mm)
                    if len(pend_stats) > 1:
                        pend_stats.pop(0)()
                while pend_stats:
                    pend_stats.pop(0)()
                P.op("act", lambda e: e.activation(out=mean[:, :n], in_=ps[sb_][:, :n], func=AF.Identity, scale=1.0 / 1024),
                     reads=[PS(sb_)], writes=[B("mean")])
                P.op("dve", lambda e: e.tensor_tensor(out=crs[:, :n], in0=mean[:, :n], in1=mean[:, :n], op=ALU.mult),
                     reads=[B("mean")], writes=[B("crs")])
                P.op("dve", lambda e: e.scalar_tensor_tensor(out=crs[:, :n], in0=ps[qb_][:, :n], scalar=1.0 / 1024, in1=crs[:, :n],
                                                             op0=ALU.mult, op1=ALU.subtract),
                     reads=[PS(qb_), B("crs")], writes=[B("crs")])
                P.op("act", lambda e: e.activation(out=crs[:, :n], in_=crs[:, :n], func=AF.Sqrt, bias=eps_t[:]), reads=[B("crs")], writes=[B("crs")])
                P.op("dve", lambda e: e.reciprocal(out=crs[:, :n], in_=crs[:, :n]), reads=[B("crs")], writes=[B("crs")])
                for c in range(8):
                    cp = convp[:, l * 8 + c, :]
                    P.op("dve", lambda e: e.tensor_tensor(out=acc[:, c, :n], in0=acc[:, c, :n], in1=mean[:, :n], op=ALU.subtract),
                         reads=[B("acc", gi % 2, c), B("mean")], writes=[B("acc", gi % 2, c)])
                    P.op("dve", lambda e: e.tensor_tensor(out=acc[:, c, :n], in0=acc[:, c, :n], in1=crs[:, :n], op=ALU.mult),
                         reads=[B("acc", gi % 2, c), B("crs")], writes=[B("acc", gi % 2, c)])
                    P.op("act", lambda e: e.activation(out=H[:, 8 + c, t0:t0 + n], in_=acc[:, c, :n], func=AF.Silu,
                                                       scale=cp[:, 32:33], bias=cp[:, 33:34]),
                         reads=[B("acc", gi % 2, c), B("convp")], writes=[B("BIG", g)])
            P.barrier()

            A = Carver(arena, S0, ARENA_BYTES)
            wsl = [A.take([128, 16, 512], BF16) for _ in range(2)]
            xt = [A.take([128, 512], F32) for _ in range(8)]
            wo = wview(w_out[l])
            NPB = 8
            if l == 0:
                NPB = 7
                feed_begin(0, 52, 96, [A.take([128, 16, 128], BF16) for _ in range(2)], 7)

            def d_load(s):
                i = s % 2
                P.op("pool", lambda e: e.dma_start(out=wsl[i], in_=wo[:, :, s * 512:(s + 1) * 512]), writes=[B("ws", i)], dma=f"ws{i}")
            d_load(0)
            units = [(s_, j, g, t0, n) for s_ in range(4) for j in range(4) for (g, t0, n) in GCD]

            def x_load(u):
                s_, j, g, t0, n = units[u]
                dc = 4 * s_ + j
                xi = u % 8
                P.op("sp", lambda e: e.dma_start(out=xt[xi][:, :n], in_=xs[dc, :, t0:t0 + n]),
                     reads=[B("xs", dc)], writes=[B("xt", xi)], dma=f"xt{xi}")
            for u0 in range(4):
                x_load(u0)
            for u, (s_, j, g, t0, n) in enumerate(units):
                if j == 0 and g == 0 and s_ + 1 < 4:
                    d_load(s_ + 1)
                if u + 4 < len(units):
                    x_load(u + 4)
                i = s_ % 2
                dc = 4 * s_ + j
                v = vmod(g)
                xi = u % 8
                pb = u % NPB
                if l == 0:
                    feed_step()

                def mmO(e):
                    for kc in range(16):
                        ins = e.matmul(ps[pb][:, :n], lhsT=wsl[i][:, kc, j * 128:(j + 1) * 128], rhs=H[:, kc, t0:t0 + n],
                                       start=(kc == 0), stop=(kc == 15))
                    return ins
                P.op("pe", mmO, reads=[B("ws", i), B("BIG", g)], writes=[PS(pb)])
                P.op("dve", lambda e: e.scalar_tensor_tensor(
                    out=xt[xi][:, :n], in0=ps[pb][:, :n], scalar=mod[:, 32 + dc, v:v + 1], in1=xt[xi][:, :n], op0=ALU.mult, op1=ALU.add),
                    reads=[PS(pb), B("xt", xi), bmod], writes=[B("xt", xi)])
                P.op("sp", lambda e: e.dma_start(out=xs[dc, :, t0:t0 + n], in_=xt[xi][:, :n]),
                     reads=[B("xt", xi)], writes=[B("xsD", dc, g)], dma=f"xo{xi}")
            if l == 0:
                feed_finish()
                ada_gmod(0, parts=(1,))
            P.barrier()

            XACC = Carver(arena, 0, 65536).take([128, 16, 1024], F32)
            H2 = Carver(arena, 65536, 98304).take([128, 16, 1024], BF16)
            A = Carver(arena, 98304, ARENA_BYTES)
            W1 = [A.take([128, 16, 512], BF16) for _ in range(2)]
            W2 = [A.take([128, 4, 2048], BF16) for _ in range(2)]
            hid = [A.take([128, 4, 512], BF16) for _ in range(2)]
            rt = [A.take([128, 512], BF16) for _ in range(2)]
            sq = [A.take([128, 512], BF16) for _ in range(2)]
            rstd = A.take([128, 1024], F32)
            tmpn = A.take([128, 1024], F32)
            tail = Carver(arena, A.off, ARENA_BYTES)
            if last:
                yt = [tail.take([128, 512], F32) for _ in range(2)]
            else:
                wa = [tail.take([128, 16, 128], BF16) for _ in range(2)]
                wav = wview(w_ada[l + 1])
            w1v = wview(w1_d[l])
            w2v = w2_d[l].rearrange("(hs p) c -> p hs c", p=128)
            mctr = [0, 0, 0, 0, 0]
            def mk_subs(ranges):
                subs_, o_ = [], 0
                for (lo, hi, v_) in ranges:
                    m_ = hi - lo
                    parts = [m_] if m_ <= 512 else [m_ // 2, m_ - m_ // 2]
                    c_ = lo
                    for p_ in parts:
                        subs_.append((o_, p_, v_, c_))
                        o_ += p_
                        c_ += p_
                assert o_ <= 1024
                return subs_
            if last:
                sglist = [(0, mk_subs([(0, 1024, 0)])), (1024, mk_subs([(1024, 2048, 0)]))]
            else:
                lat = 2048 + nCD
                a_ = min(1024, -(-(lat + 256) // 3 // 64) * 64)
                sglist = [(0, mk_subs([(0, a_, 0)])), (a_, mk_subs([(a_, 2 * a_, 0)])),
                          (2 * a_, mk_subs([(2 * a_, lat, 0), (2560, 2816, 1)]))]
                assert sum(len(sg[1]) for sg in sglist) * 16 >= 96
                assert len(sglist[0][1]) == 2 and len(sglist[1][1]) == 2
            sbank = [6, 7, 5]

            def wa_load(oc):
                i = oc % 2
                P.op("pool", lambda e: e.dma_start(out=wa[i], in_=wav[:, :, oc * 128:(oc + 1) * 128]), writes=[B("wa", i)], dma=f"wa{i}")
            unit = 0
            if not last:
                wa_load(0)
            for sgi, (s0, subs) in enumerate(sglist):

                def m_load(hsg):
                    i = hsg % 2
                    P.op("pool", lambda e: e.dma_start(out=W1[i], in_=w1v[:, :, hsg * 512:(hsg + 1) * 512]), writes=[B("W1", i)], dma=f"W1{i}")
                    P.op("pool", lambda e: e.dma_start(out=W2[i], in_=w2v[:, hsg * 4:(hsg + 1) * 4, :]), writes=[B("W2", i)], dma=f"W2{i}")
                m_load(0)

                def stats(final):
                    for c in range(16):
                        if not final:
                            def ld(e):
                                for (o, n, v, col) in subs:
                                    ins = e.dma_start(out=XACC[:, c, o:o + n], in_=xs[c, :, col:col + n])
                                return ins
                            P.op("sp", ld, reads=[B("xs", c)], writes=[B("XACC", c)], dma=f"xacc{c}")
                        for k, (o, n, v, col) in enumerate(subs):
                            i2 = mctr[0] % 2
                            mctr[0] += 1
                            P.op("act", lambda e: e.activation(out=sq[i2][:, :n], in_=XACC[:, c, o:o + n], func=AF.Square),
                                 reads=[B("XACC", c)], writes=[B("sq", i2)])
                            P.op("pe", lambda e: e.matmul(ps[sbank[k]][:, :n], lhsT=ones_b[:], rhs=sq[i2][:, :n], start=(c == 0), stop=(c == 15)),
                                 reads=[B("sq", i2), B("ones")], writes=[PS(sbank[k])])
                    for k, (o, n, v, col) in enumerate(subs):
                        P.op("act", lambda e: e.activation(out=rstd[:, o:o + n], in_=ps[sbank[k]][:, :n], func=AF.Sqrt, scale=1.0 / D, bias=eps_t[:]),
                             reads=[PS(sbank[k])], writes=[B("rstd")])
                        P.op("dve", lambda e: e.reciprocal(out=rstd[:, o:o + n], in_=rstd[:, o:o + n]), reads=[B("rstd")], writes=[B("rstd")])

                stats(False)
                for c in range(16):
                    for k, (o, n, v, col) in enumerate(subs):
                        P.op("dve", lambda e: e.tensor_tensor(out=tmpn[:, o:o + n], in0=XACC[:, c, o:o + n], in1=rstd[:, o:o + n], op=ALU.mult),
                             reads=[B("XACC", c), B("rstd")], writes=[B("tmpn", k)])
                        P.op("act", lambda e: e.activation(out=H2[:, c, o:o + n], in_=tmpn[:, o:o + n], func=AF.Identity,
                                                           scale=gmod[:, 1, c, v:v + 1], bias=mod[:, 48 + c, v:v + 1]),
                             reads=[B("tmpn", k), bgmod, bmod], writes=[B("H2", k)])
                def stage2(hsg, k, hi):
                    o, n, v, col = subs[k]
                    wi_ = hsg % 2
                    for dc in range(16):
                        yb = 3 + mctr[4] % 3
                        mctr[4] += 1

                        def mm2(e):
                            for j in range(4):
                                ins = e.matmul(ps[yb][:, :n], lhsT=W2[wi_][:, j, dc * 128:(dc + 1) * 128], rhs=hid[hi][:, j, :n],
                                               start=(j == 0), stop=(j == 3))
                            return ins
                        P.op("pe", mm2, reads=[B("W2", wi_), B("hid", hi)], writes=[PS(yb)])
                        P.op("dve", lambda e: e.scalar_tensor_tensor(
                            out=XACC[:, dc, o:o + n], in0=ps[yb][:, :n], scalar=mod[:, 80 + dc, v:v + 1], in1=XACC[:, dc, o:o + n],
                            op0=ALU.mult, op1=ALU.add),
                            reads=[PS(yb), B("XACC", dc), bmod], writes=[B("XACC", dc)])

                pend2 = None
                for hsg in range(16):
                    wi_ = hsg % 2
                    for k, (o, n, v, col) in enumerate(subs):
                        if not last and unit < 96:
                            oc = unit
                            unit += 1
                            if oc + 1 < 96:
                                wa_load(oc + 1)
                            ada_mm(l + 1, oc, wa[oc % 2], 0, B("wa", oc % 2), 7)
                            if oc % 4 == 3:
                                ada_evac(l + 1, oc // 4, 7)
                        hi = mctr[1] % 2
                        mctr[1] += 1
                        for j in range(4):
                            hb = mctr[2] % 3
                            mctr[2] += 1
                            ri = mctr[3] % 2
                            mctr[3] += 1

                            def mm1(e):
                                for kc in range(16):
                                    ins = e.matmul(ps[hb][:, :n], lhsT=W1[wi_][:, kc, j * 128:(j + 1) * 128], rhs=H2[:, kc, o:o + n],
                                                   start=(kc == 0), stop=(kc == 15))
                                return ins
                            P.op("pe", mm1, reads=[B("W1", wi_), B("H2", k)], writes=[PS(hb)])
                            P.op("act", lambda e: e.activation(out=rt[ri][:, :n], in_=ps[hb][:, :n], func=AF.Relu),
                                 reads=[PS(hb)], writes=[B("rt", ri)])
                            P.op("dve", lambda e: e.tensor_tensor(out=hid[hi][:, j, :n], in0=ps[hb][:, :n], in1=rt[ri][:, :n], op=ALU.mult),
                                 reads=[PS(hb), B("rt", ri)], writes=[B("hid", hi)])
                        if pend2 is not None:
                            stage2(*pend2)
                        pend2 = (hsg, k, hi)
                        if k == 0 and hsg + 1 < 16:
                            m_load(hsg + 1)
                if pend2 is not None:
                    stage2(*pend2)
                if not last:
                    for c in range(16):
                        def stx(e):
                            for (o, n, v, col) in subs:
                                ins = e.dma_start(out=xs[c, :, col:col + n], in_=XACC[:, c, o:o + n])
                            return ins
                        P.op("sp", stx, reads=[B("XACC", c)], writes=[B("xs", c)], dma=f"xst{c}")
                else:
                    stats(True)
                    for c in range(16):
                        for k, (o, n, v, col) in enumerate(subs):
                            yi = mctr[0] % 2
                            mctr[0] += 1
                            P.op("dve", lambda e: e.tensor_tensor(out=tmpn[:, o:o + n], in0=XACC[:, c, o:o + n], in1=rstd[:, o:o + n], op=ALU.mult),
                                 reads=[B("XACC", c), B("rstd")], writes=[B("tmpn", k)])
                            P.op("act", lambda e: e.activation(out=yt[yi][:, :n], in_=tmpn[:, o:o + n], func=AF.Identity,
                                                               scale=gvec[:, 2 * L, c:c + 1]),
                                 reads=[B("tmpn", k), B("gvec")], writes=[B("yt", yi)])
                            P.op("sp", lambda e: e.dma_start(out=y_d[c, :, col:col + n], in_=yt[yi][:, :n]),
                                 reads=[B("yt", yi)], writes=[B("yout")], dma=f"yo{yi}")
            if not last:
                assert unit == 96
                ada_gmod(l + 1)
            P.barrier()
        P.wait_all("sp", [B("yout")])
        P.emit()
    return nc


def _rope_tables(pos):
    pos = np.asarray(pos, dtype=np.int64)
    row = (pos // 64).astype(np.float32)
    col = (pos % 64).astype(np.float32)
    n_freq = 32
    inv = (np.float32(10000.0) ** (-np.arange(n_freq, dtype=np.float32) / np.float32(n_freq))).astype(np.float32)
    theta = np.concatenate([row[:, None] * inv, col[:, None] * inv], axis=-1)
    theta = np.concatenate([theta, theta], axis=-1)
    return np.cos(theta).astype(np.float32), np.sin(theta).astype(np.float32)


def _fm(a2d):
    t = a2d.shape[0]
    return np.ascontiguousarray(a2d.T.reshape(16, 128, t))


def _vec_fm(v, nch):
    v = np.asarray(v)
    lead = v.shape[:-1]
    r = v.reshape(lead + (nch, 128))
    return np.ascontiguousarray(np.moveaxis(r, -1, 0))


def prepare_shared(inp, L):
    f = lambda a: np.ascontiguousarray(np.asarray(a, dtype=np.float32))
    sh = {}
    w_in = np.asarray(inp["w_in"])[:L]
    sh["w_in"] = f(w_in)
    sh["w_ada"] = f(np.asarray(inp["w_ada"])[:L])
    b = _vec_fm(np.asarray(inp["b_ada"])[:L], 96)
    sh["b_ada2"] = np.ascontiguousarray(np.repeat(b[:, :, :, None], 2, axis=3).reshape(128, L * 96 * 2))
    gv = np.concatenate([np.asarray(inp["g_mix"])[:L], np.asarray(inp["g_mlp"])[:L], np.asarray(inp["g_final"])[None]], axis=0)
    sh["gvec"] = np.ascontiguousarray(_vec_fm(gv, 16).reshape(128, (2 * L + 1) * 16))
    sink = np.asarray(inp["attn_sink"])[:L].reshape(L, 2, 4)
    sh["sinkrow"] = np.ascontiguousarray(np.repeat(sink[:, :, :, None], 128, axis=3).reshape(1, L * 2 * 512)).astype(np.float32)
    sh["w_out"] = f(np.asarray(inp["w_out"])[:L])
    sh["w1"] = f(np.asarray(inp["w_mlp1"])[:L])
    sh["w2"] = f(np.asarray(inp["w_mlp2"])[:L])
    ident = np.eye(128, dtype=np.float32)
    j = np.arange(128)[:, None]
    r = np.arange(128)[None, :]
    mL = np.where(j >= r, 0.0, -30000.0).astype(np.float32)
    mR = np.where(j <= r, 0.0, -30000.0).astype(np.float32)
    sh["cst"] = np.ascontiguousarray(np.concatenate([ident, np.tile(mL, (1, 4)), np.tile(mR, (1, 4))], axis=1))
    cw = np.asarray(inp["conv_w"])[:L]
    sh["_conv"] = (cw, np.asarray(inp["conv_b"])[:L], np.asarray(inp["conv_ln_g"])[:L], np.asarray(inp["conv_ln_b"])[:L])
    return sh


def prepare_core(inp, sh, core, L):
    b, half = core // 2, core % 2
    x = np.asarray(inp["x"])[b]
    if half == 0:
        pos = np.arange(0, NLAT)
    else:
        pos = np.arange(4095, 4095 - NLAT, -1)
    xl = x[pos]
    ctx = np.asarray(inp["ctx"])[b]
    m = {k: v for k, v in sh.items() if not k.startswith("_")}
    m["xin"] = _fm(np.concatenate([xl, ctx], axis=0))
    cc = np.stack([np.asarray(inp["c"])[b], np.asarray(inp["c_ctx"])], axis=-1)
    m["cT"] = np.ascontiguousarray(cc.reshape(16, 128, 2).transpose(1, 0, 2)).astype(np.float32)
    cos, sin = _rope_tables(pos)
    sgn = np.concatenate([-np.ones(64, np.float32), np.ones(64, np.float32)])
    cosT = np.concatenate([cos.T, np.ones((128, NCTX), np.float32)], axis=1)
    sinT = np.concatenate([(sin * sgn[None, :]).T, np.zeros((128, NCTX), np.float32)], axis=1)
    m["rope"] = np.ascontiguousarray(np.stack([cosT, sinT], axis=1)).astype(np.float32)
    cw, cb, lg, lb = sh["_conv"]
    if half == 1:
        cw = cw[:, ::-1, :]
    taps = np.moveaxis(cw.reshape(L, 31, 8, 128), (3, 0, 2, 1), (0, 1, 2, 3))
    extra = np.stack([_vec_fm(cb, 8), _vec_fm(lg, 8), _vec_fm(lb, 8)], axis=-1)
    m["convp"] = np.ascontiguousarray(np.concatenate([taps, extra], axis=-1).reshape(128, L * 8 * 34)).astype(np.float32)
    wp = np.concatenate([cw, np.zeros((L, 1, 1024), cw.dtype)], axis=1).reshape(L, 8, 4, 8, 4, 32)
    T = np.zeros((4, 32, L, 8, 8, 4, 32), np.float32)
    for c_ in range(32):
        T[:, c_, :, :, :, :, c_] = wp[..., c_].transpose(2, 0, 3, 1, 4)
    m["convw"] = np.ascontiguousarray(T.reshape(128, L * 8192))
    return m


def assemble(results, n_cores_run, cores):
    out = np.zeros((4, 4096, D), np.float32)
    for core, res in zip(cores, results):
        b, half = core // 2, core % 2
        yt = res["y"].reshape(D, NOWN).T
        if half == 0:
            out[b, 0:NOWN] = yt
        else:
            out[b, 4095 - np.arange(NOWN)] = yt
    return out


_CACHE = {}


def kernel(**inputs):
    L = 4
    if L not in _CACHE:
        _CACHE[L] = build_program(L)
    nc = _CACHE[L]
    sh = prepare_shared(inputs, L)
    cores = list(range(8))
    in_maps = [prepare_core(inputs, sh, c, L) for c in cores]
    res = run_bass_kernel_spmd(nc, in_maps, core_ids=cores)
    return assemble(res.results, 8, cores)
```

```python
import contextlib
import numpy as np
import concourse.bass as bass
import concourse.mybir as mybir
from concourse.bass_utils import run_bass_kernel_spmd

F32 = mybir.dt.float32
BF16 = mybir.dt.bfloat16
AF = mybir.ActivationFunctionType
ALU = mybir.AluOpType

D = 2048
NLAT = 2560
NCTX = 256
NT = NLAT + NCTX
NOWN = 2048
HCW = 2880
GROUPS = [(0, 512), (512, 512), (1024, 512), (1536, 512), (2048, 512), (2560, 256)]
SGS = [(0, 1024, [0, 1]), (1024, 1024, [2, 3]), (2048, 768, [4, 5])]
EPS = 1e-6
SCALE = 128.0 ** -0.5


class Buf:
    __slots__ = ("name", "w", "r")

    def __init__(self, name):
        self.name = name
        self.w = None
        self.r = {}


class Rec:
    def __init__(self):
        self.calls = []

    def __getattr__(self, name):
        def f(*a, **k):
            self.calls.append((name, a, k))
            return self
        return f


class Prog:
    ENGS = ("pe", "act", "dve", "pool", "sp")

    def __init__(self, nc, stack):
        self.nc = nc
        self.stack = stack
        self.ops = {e: [] for e in self.ENGS}
        self.cnt = {}
        self.sems = {}
        self.seen = {e: {} for e in self.ENGS}
        self.bufs = {}
        for e in self.ENGS:
            self._sem(e)

    def B(self, *key):
        b = self.bufs.get(key)
        if b is None:
            b = self.bufs[key] = Buf(key)
        return b

    def _sem(self, key):
        if key not in self.sems:
            self.sems[key] = self.stack.enter_context(self.nc.semaphore("s_" + key))
            self.cnt[key] = 0
        return self.sems[key]

    def op(self, eng, fn, reads=(), writes=(), dma=None):
        deps = {}
        for b in reads:
            if b.w is not None and deps.get(b.w[0], 0) < b.w[1]:
                deps[b.w[0]] = b.w[1]
            if b.name[0] == "ps":
                for k, v in b.r.items():
                    if k != eng and deps.get(k, 0) < v:
                        deps[k] = v
        for b in writes:
            if b.w is not None and deps.get(b.w[0], 0) < b.w[1]:
                deps[b.w[0]] = b.w[1]
            for k, v in b.r.items():
                if deps.get(k, 0) < v:
                    deps[k] = v
        if dma is None:
            key = eng
            inc = 1
        else:
            key = "d_" + dma
            self._sem(key)
            inc = 16
        self.cnt[key] += inc
        tick = (key, self.cnt[key])
        waits = []
        for k, v in deps.items():
            if k == eng:
                if eng == "pe":
                    continue
                if dma is None and self.cnt[eng] - v > 3:
                    continue
            if self.seen[eng].get(k, 0) >= v:
                continue
            self.seen[eng][k] = v
            waits.append((k, v))
        for b in reads:
            if b.r.get(tick[0], 0) < tick[1]:
                b.r[tick[0]] = tick[1]
        for b in writes:
            b.w = tick
            b.r = {}
        rec = Rec()
        fn(rec)
        if dma is not None and len(rec.calls) > 1:
            extra = 16 * (len(rec.calls) - 1)
            self.cnt[key] += extra
            tick = (key, self.cnt[key])
            for b in reads:
                b.r[tick[0]] = tick[1]
            for b in writes:
                b.w = tick
        self.ops[eng].append((waits, rec.calls, key, inc))

    def wait_all(self, eng, bufs):
        waits = []
        for b in bufs:
            items = list(b.r.items())
            if b.w is not None:
                items.append(b.w)
            for k, v in items:
                if self.seen[eng].get(k, 0) >= v:
                    continue
                self.seen[eng][k] = v
                waits.append((k, v))
        self.ops[eng].append((waits, None, None, 0))

    def barrier(self):
        snap = {k: v for k, v in self.cnt.items() if v > 0 and k != "sp"}
        for e in self.ENGS:
            waits = []
            for k, v in snap.items():
                if k == e:
                    continue
                if self.seen[e].get(k, 0) >= v:
                    continue
                self.seen[e][k] = v
                waits.append((k, v))
            if waits:
                self.ops[e].append((waits, None, None, 0))

    def emit(self):
        nc = self.nc
        with nc.Block() as block:
            def run(name):
                def body(e):
                    for waits, fn, key, inc in self.ops[name]:
                        for k, v in waits:
                            e.wait_ge(self.sems[k], v)
                        if fn is not None:
                            for name_, a, k in fn:
                                ins = getattr(e, name_)(*a, **k)
                                if inc == 16:
                                    ins.then_inc(self.sems[key], 16)
                            if inc != 16:
                                ins.then_inc(self.sems[key], inc)
                return body
            block.tensor(run("pe"))
            block.scalar(run("act"))
            block.vector(run("dve"))
            block.gpsimd(run("pool"))
            block.sync(run("sp"))


class Carver:
    def __init__(self, arena, base, limit):
        self.arena = arena
        self.off = base
        self.limit = limit

    def take(self, shape, dt):
        n = 1
        for s in shape[1:]:
            n *= s
        nb = n * (2 if dt == BF16 else 4)
        nb = (nb + 31) // 32 * 32
        assert self.off + nb <= self.limit, (self.off, nb, self.limit)
        v = self.arena[:, self.off // 2:(self.off + nb) // 2]
        if dt == F32:
            v = v.bitcast(F32)
        v = v[:, 0:n]
        if len(shape) == 3:
            v = v.rearrange("p (a b) -> p a b", a=shape[1])
        self.off += nb
        return v


ARENA_BYTES = 194816
S0 = 112640


def build_program(L):
    nc = bass.Bass("TRN2", target_bir_lowering=False)

    def din(name, shape):
        return nc.dram_tensor(name, shape, F32, kind="ExternalInput").ap()

    xin = din("xin", [16, 128, NT])
    cT_d = din("cT", [128, 16, 2])
    rope_d = din("rope", [128, 2, NT])
    cst_d = din("cst", [128, 128 + 1024])
    w_ada = din("w_ada", [L, D, 6 * D])
    bada_d = din("b_ada2", [128, L * 96 * 2])
    gvec_d = din("gvec", [128, (2 * L + 1) * 16])
    w_in = din("w_in", [L, D, 3584])
    sink_d = din("sinkrow", [1, L * 2 * 512])
    convp_d = din("convp", [128, L * 8 * 34])
    convw_d = din("convw", [128, L * 8192])
    w_out = din("w_out", [L, D, D])
    w1_d = din("w1", [L, D, 4 * D])
    w2_d = din("w2", [L, 4 * D, D])
    y_d = nc.dram_tensor("y", [16, 128, NOWN], F32, kind="ExternalOutput").ap()
    xs = nc.dram_tensor("xs", [16, 128, NT], F32).ap()
    qs = nc.dram_tensor("qs", [8, 128, NT], BF16).ap()
    hcs = nc.dram_tensor("hcs", [8, 128, HCW], BF16).ap()
    dgs = nc.dram_tensor("dgs", [L * 8, 128, 31 * 128], BF16).ap()

    with contextlib.ExitStack() as st:
        P = Prog(nc, st)
        B = P.B

        def sb(name, shape, dt):
            return st.enter_context(nc.sbuf_tensor(name, shape, dt))

        arena = sb("arena", [128, ARENA_BYTES // 2], BF16)
        ones_f = sb("ones_f", [128, 128], F32)
        ones_b = sb("ones_b", [128, 128], BF16)
        ident_b = sb("ident_b", [128, 128], BF16)
        perm_b = sb("perm_b", [128, 128], BF16)
        mask_b = sb("mask_b", [128, 2, 512], BF16)
        eps_t = sb("eps_t", [128, 1], F32)
        silu_c = sb("silu_c", [128, 16, 2], BF16)
        mods = [sb(f"mod{i}", [128, 96, 2], F32) for i in range(2)]
        gmods = [sb(f"gmod{i}", [128, 2, 16, 2], F32) for i in range(2)]
        badat = sb("badat", [128, L * 96, 2], F32)
        gvec = sb("gvec_t", [128, 2 * L + 1, 16], F32)
        esink = sb("esink", [1, 2 * 512], BF16)
        convp = sb("convp_t", [128, L * 8, 34], F32)
        ps = [st.enter_context(nc.psum_tensor(f"ps{i}", [128, 512], F32)) for i in range(8)]

        def PS(i):
            return B("ps", i)

        H = Carver(arena, 0, 90112).take([128, 16, NT], BF16)
        KT = Carver(arena, 90112, 101376).take([128, 2, NT], BF16)
        V = Carver(arena, 101376, 112640).take([128, 22, 256], BF16)

        def gof(t):
            return 5 if t >= NLAT else t // 512

        def vmod(g):
            return 1 if g == 5 else 0

        A = Carver(arena, S0, ARENA_BYTES)
        cst_f = A.take([128, 1152], F32)
        cT_f = A.take([128, 32], F32)
        zt = A.take([128, 8, 32], BF16)
        P.op("sp", lambda e: e.dma_start(out=cst_f, in_=cst_d), writes=[B("cst_f")], dma="p0")
        P.op("sp", lambda e: e.dma_start(out=cT_f, in_=cT_d.rearrange("p a b -> p (a b)")), writes=[B("cT_f")], dma="p1")
        P.op("sp", lambda e: e.dma_start(out=badat[:].rearrange("p a b -> p (a b)"), in_=bada_d), writes=[B("bada")], dma="p3")
        P.op("sp", lambda e: e.dma_start(out=gvec[:].rearrange("p a b -> p (a b)"), in_=gvec_d), writes=[B("gvec")], dma="p4")
        P.op("sp", lambda e: e.dma_start(out=convp[:].rearrange("p a b -> p (a b)"), in_=convp_d), writes=[B("convp")], dma="p5")
        for c in range(16):
            P.op("sp", lambda e, c=c: e.dma_start(out=xs[c], in_=xin[c]), writes=[B("xs", c)], dma="p6")
        P.op("dve", lambda e: e.memset(ones_f[:], 1.0), writes=[B("ones")])
        P.op("dve", lambda e: e.memset(ones_b[:], 1.0), writes=[B("ones")])
        P.op("dve", lambda e: e.memset(eps_t[:], EPS), writes=[B("ones")])
        P.op("dve", lambda e: e.memset(zt, 0.0), writes=[B("zt")])
        P.op("dve", lambda e: e.tensor_copy(out=ident_b[:], in_=cst_f[:, 0:128]), reads=[B("cst_f")], writes=[B("ones")])
        P.op("dve", lambda e: e.tensor_copy(out=perm_b[:, 0:64], in_=cst_f[:, 64:128]), reads=[B("cst_f")], writes=[B("ones")])
        P.op("dve", lambda e: e.tensor_copy(out=perm_b[:, 64:128], in_=cst_f[:, 0:64]), reads=[B("cst_f")], writes=[B("ones")])
        P.op("dve", lambda e: e.tensor_copy(out=mask_b[:].rearrange("p a b -> p (a b)"), in_=cst_f[:, 128:1152]),
             reads=[B("cst_f")], writes=[B("ones")])
        P.op("act", lambda e: e.activation(out=silu_c[:].rearrange("p a b -> p (a b)"), in_=cT_f, func=AF.Silu),
             reads=[B("cT_f")], writes=[B("siluc")])
        hv = hcs.rearrange("c p t -> p c t")
        P.op("sp", lambda e: e.dma_start(out=hv[:, :, 0:16], in_=zt[:, :, 0:16]), reads=[B("zt")], writes=[B("hcs_pad")], dma="p7")
        P.op("sp", lambda e: e.dma_start(out=hv[:, :, 2576:2608], in_=zt[:, :, 0:32]), reads=[B("zt")], writes=[B("hcs_pad")], dma="p7")
        P.op("sp", lambda e: e.dma_start(out=hv[:, :, 2864:2880], in_=zt[:, :, 0:16]), reads=[B("zt")], writes=[B("hcs_pad")], dma="p7")
        P.barrier()

        def wview(w2d):
            return w2d.rearrange("(kc p) c -> p kc c", p=128)

        def ada_mm(lx, oc, wt, col, wbuf, bank):
            def mm(e):
                for kc in range(16):
                    ins = e.matmul(ps[bank][:, 2 * (oc % 4):2 * (oc % 4) + 2], lhsT=wt[:, kc, col:col + 128],
                                   rhs=silu_c[:, kc, :], start=(kc == 0), stop=(kc == 15))
                return ins
            P.op("pe", mm, reads=[wbuf, B("siluc")], writes=[PS(bank)])

        def ada_evac(lx, s4, bank):
            P.op("dve", lambda e: e.tensor_tensor(
                out=mods[lx % 2][:, 4 * s4:4 * s4 + 4, :], in0=ps[bank][:, 0:8].rearrange("p (j v) -> p j v", v=2),
                in1=badat[:, lx * 96 + 4 * s4:lx * 96 + 4 * s4 + 4, :], op=ALU.add),
                reads=[PS(bank), B("bada")], writes=[B("mod", lx % 2)])

        def ada_gmod(lx, parts=(0, 1)):
            m_, g_ = mods[lx % 2], gmods[lx % 2]
            for v in range(2):
                if 0 in parts:
                    P.op("dve", lambda e: e.scalar_tensor_tensor(out=g_[:, 0, :, v], in0=m_[:, 16:32, v], scalar=1.0,
                                                                 in1=gvec[:, lx, :], op0=ALU.add, op1=ALU.mult),
                         reads=[B("mod", lx % 2), B("gvec")], writes=[B("gmod", lx % 2)])
                if 1 in parts:
                    P.op("dve", lambda e: e.scalar_tensor_tensor(out=g_[:, 1, :, v], in0=m_[:, 64:80, v], scalar=1.0,
                                                                 in1=gvec[:, L + lx, :], op0=ALU.add, op1=ALU.mult),
                         reads=[B("mod", lx % 2), B("gvec")], writes=[B("gmod", lx % 2)])

        feed = {"oc": 96, "end": 96}

        def feed_load(oc):
            i = oc % 2
            P.op("pool", lambda e: e.dma_start(out=feed["wa"][i], in_=feed["wav"][:, :, oc * 128:(oc + 1) * 128]),
                 writes=[B("wa", i)], dma=f"wa{i}")

        def feed_begin(lx, oc0, oc1, wa_tiles, bank):
            feed.update(lx=lx, oc=oc0, end=oc1, wa=wa_tiles, bank=bank, wav=wview(w_ada[lx]))
            feed_load(oc0)

        def feed_step():
            if feed["oc"] >= feed["end"]:
                return
            oc = feed["oc"]
            feed["oc"] += 1
            if oc + 1 < feed["end"]:
                feed_load(oc + 1)
            ada_mm(feed["lx"], oc, feed["wa"][oc % 2], 0, B("wa", oc % 2), feed["bank"])
            if oc % 4 == 3:
                ada_evac(feed["lx"], oc // 4, feed["bank"])

        def feed_finish():
            while feed["oc"] < feed["end"]:
                feed_step()

        def do_esink(lx, sink_f):
            P.op("sp", lambda e: e.dma_start(out=sink_f[0:1, :], in_=sink_d[:, lx * 1024:(lx + 1) * 1024]), writes=[B("sink_f")], dma="p2")
            P.op("act", lambda e: e.activation(out=esink[:], in_=sink_f[0:1, :], func=AF.Exp), reads=[B("sink_f")], writes=[B("esink")])

        for l in range(L):
            last = l == L - 1
            rr = L - 1 - l
            nAB = 128 * (rr + 1)
            nCD = 128 * rr
            GAB = [(g, GROUPS[g][0], GROUPS[g][1]) for g in range(4)] + [(4, 2048, nAB), (5, 2560, 256)]
            GCD = [(g, GROUPS[g][0], GROUPS[g][1]) for g in range(4)]
            if nCD > 0:
                GCD.append((4, 2048, nCD))
            if not last:
                GCD.append((5, 2560, 256))
            kb_max = 15 + nAB // 128
            mod = mods[l % 2]
            gmod = gmods[l % 2]
            bmod = B("mod", l % 2)
            bgmod = B("gmod", l % 2)
            if l == 0:
                A = Carver(arena, S0, ARENA_BYTES)
                wsl = [A.take([128, 16, 512], BF16) for _ in range(3)]
                sink_f = A.take([128, 1024], F32)
                dgt = [A.take([128, 31 * 128], BF16) for _ in range(2)]
                do_esink(0, sink_f)
                wv = wview(w_ada[0])

                def ada_load(s_):
                    i = s_ % 3
                    P.op("pool", lambda e: e.dma_start(out=wsl[i], in_=wv[:, :, s_ * 512:(s_ + 1) * 512]),
                         writes=[B("ws", i)], dma=f"ws{i}")

                def mkdiag_set(lc):
                    di = lc % 2

                    def mkdiag(e):
                        for k in range(31):
                            ins = e.tensor_scalar_mul(out=dgt[di][:, k * 128:(k + 1) * 128], in0=ident_b[:], scalar1=convp[:, lc, k:k + 1])
                        return ins
                    P.op("dve", mkdiag, reads=[B("ones"), B("convp")], writes=[B("dgt", di)])
                    P.op("sp", lambda e: e.dma_start(out=dgs[lc], in_=dgt[di]), reads=[B("dgt", di)], writes=[B("dgs")], dma=f"dgt{di}")

                NSL = 8
                ada_load(0)
                ada_load(1)
                nsets = 0
                for s_ in range(NSL):
                    if s_ + 2 < NSL:
                        ada_load(s_ + 2)
                    i = s_ % 3
                    bank = s_ % 2
                    for j in range(4):
                        ada_mm(0, 4 * s_ + j, wsl[i], j * 128, B("ws", i), bank)
                    ada_evac(0, s_, bank)
                ada_gmod(0, parts=(0,))
                P.barrier()

            A = Carver(arena, S0, ARENA_BYTES)
            xg = [A.take([128, 16, 512], F32) for _ in range(2)]
            sqb = [A.take([128, 512], BF16) for _ in range(4)]
            tmp = [A.take([128, 512], F32) for _ in range(2)]
            rstd = A.take([128, 512], F32)
            if l > 0:
                do_esink(l, A.take([128, 1024], F32))
            ctr = [0, 0]
            for g, t0, n in GAB:
                v = vmod(g)
                gi_ = g % 2
                P.op("sp", lambda e: e.dma_start(out=xg[gi_][:, :, :n], in_=xs[:, :, t0:t0 + n].rearrange("c p t -> p c t")),
                     reads=[B("xs", c) for c in range(16)], writes=[B("xg", gi_)], dma=f"xg{gi_}")
                for c in range(16):
                    i2 = ctr[0] % 4
                    ctr[0] += 1
                    P.op("dve" if c % 4 == 3 else "pool", lambda e: e.tensor_tensor(out=sqb[i2][:, :n], in0=xg[gi_][:, c, :n], in1=xg[gi_][:, c, :n], op=ALU.mult),
                         reads=[B("xg", gi_)], writes=[B("sq", i2)])
                    P.op("pe", lambda e: e.matmul(ps[7][:, :n], lhsT=ones_b[:], rhs=sqb[i2][:, :n], start=(c == 0), stop=(c == 15)),
                         reads=[B("sq", i2), B("ones")], writes=[PS(7)])
                P.op("act", lambda e: e.activation(out=rstd[:, :n], in_=ps[7][:, :n], func=AF.Sqrt, scale=1.0 / D, bias=eps_t[:]),
                     reads=[PS(7)], writes=[B("rstd")])
                P.op("dve", lambda e: e.reciprocal(out=rstd[:, :n], in_=rstd[:, :n]), reads=[B("rstd")], writes=[B("rstd")])
                for c in range(16):
                    i2 = ctr[1] % 2
                    ctr[1] += 1
                    P.op("dve", lambda e: e.tensor_tensor(out=tmp[i2][:, :n], in0=xg[gi_][:, c, :n], in1=rstd[:, :n], op=ALU.mult),
                         reads=[B("xg", gi_), B("rstd")], writes=[B("tmp", i2)])
                    P.op("act", lambda e: e.activation(out=H[:, c, t0:t0 + n], in_=tmp[i2][:, :n], func=AF.Identity,
                                                       scale=gmod[:, 0, c, v:v + 1], bias=mod[:, c, v:v + 1]),
                         reads=[B("tmp", i2), bgmod, bmod], writes=[B("BIG", g)])
            P.barrier()

            A = Carver(arena, S0, ARENA_BYTES)
            wsl = [A.take([128, 16, 512], BF16) for _ in range(4)]
            rpc = Carver(arena, S0 + 16384, S0 + 32768)
            rp = [rpc.take([128, 2, 512], F32) for _ in range(4)]
            rctr = [0]
            t1 = [A.take([128, 512], F32) for _ in range(2)]
            t2 = [A.take([128, 512], F32) for _ in range(2)]
            ot = [A.take([128, 512], BF16) for _ in range(2)]
            qtmp = [A.take([128, 512], BF16) for _ in range(2)]
            wi = wview(w_in[l])
            steps = [("q", 0), ("q", 1), ("kv", 0), ("u", 0), ("u", 1)]

            def b_load(t):
                kind, i = steps[t]
                sa, sbb = 2 * (t % 2), 2 * (t % 2) + 1
                if kind == "q":
                    srcA, srcB = wi[:, :, i * 512:(i + 1) * 512], None
                elif kind == "kv":
                    srcA, srcB = wi[:, :, 1024:1536], None
                else:
                    srcA, srcB = wi[:, :, 1536 + i * 512:2048 + i * 512], wi[:, :, 2560 + i * 512:3072 + i * 512]
                P.op("pool", lambda e: e.dma_start(out=wsl[sa], in_=srcA), writes=[B("ws", sa)], dma=f"ws{sa}")
                if srcB is not None:
                    P.op("pool", lambda e: e.dma_start(out=wsl[sbb], in_=srcB),
                         writes=[B("ws", sbb)] + ([B("rp", k) for k in range(4)] if sbb == 1 else []), dma=f"ws{sbb}")

            bctr = [0, 0]
            b_load(0)
            for t, (kind, i) in enumerate(steps):
                if t + 1 < len(steps):
                    b_load(t + 1)
                sa, sbb = 2 * (t % 2), 2 * (t % 2) + 1
                nj = 2 if kind == "kv" else 4

                def rope_tail(u):
                    j, g, t0, n, pa, pb, r, rq = u
                    P.op("pe", lambda e: e.matmul(ps[pb][:, :n], lhsT=perm_b[:], rhs=qtmp[r][:, :n], start=True, stop=True),
                         reads=[B("qtmp", r), B("ones")], writes=[PS(pb)])
                    P.op("dve", lambda e: e.tensor_tensor(out=t1[r][:, :n], in0=ps[pa][:, :n], in1=rp[rq][:, 0, :n], op=ALU.mult),
                         reads=[PS(pa), B("rp", rq)], writes=[B("t1", r)])
                    P.op("dve", lambda e: e.tensor_tensor(out=t2[r][:, :n], in0=ps[pb][:, :n], in1=rp[rq][:, 1, :n], op=ALU.mult),
                         reads=[PS(pb), B("rp", rq)], writes=[B("t2", r)])
                    if kind == "q":
                        h = 4 * i + j
                        P.op("dve", lambda e: e.tensor_tensor(out=ot[r][:, :n], in0=t1[r][:, :n], in1=t2[r][:, :n], op=ALU.add),
                             reads=[B("t1", r), B("t2", r)], writes=[B("ot", r)])
                        P.op("sp", lambda e: e.dma_start(out=qs[h, :, t0:t0 + n], in_=ot[r][:, :n]),
                             reads=[B("ot", r)], writes=[B("qs", g)], dma=f"ot{r}")
                    else:
                        P.op("dve", lambda e: e.tensor_tensor(out=KT[:, j, t0:t0 + n], in0=t1[r][:, :n], in1=t2[r][:, :n], op=ALU.add),
                             reads=[B("t1", r), B("t2", r)], writes=[B("KT", g)])

                pending = None
                for j in range(nj):
                    for g, t0, n in GAB:
                        pa = 2 * (bctr[0] % 3)
                        pb = pa + 1
                        bctr[0] += 1
                        r = bctr[1] % 2
                        bctr[1] += 1

                        def mmA(e):
                            for kc in range(16):
                                ins = e.matmul(ps[pa][:, :n], lhsT=wsl[sa][:, kc, j * 128:(j + 1) * 128], rhs=H[:, kc, t0:t0 + n],
                                               start=(kc == 0), stop=(kc == 15))
                            return ins
                        P.op("pe", mmA, reads=[B("ws", sa), B("BIG", g)], writes=[PS(pa)])
                        if kind in ("q", "kv"):
                            rq = rctr[0] % 4
                            rctr[0] += 1
                            P.op("sp", lambda e: e.dma_start(out=rp[rq][:, :, :n], in_=rope_d[:, :, t0:t0 + n]),
                                 reads=[B("ws", 1)], writes=[B("rp", rq)], dma=f"rp{rq}")
                            P.op("act", lambda e: e.activation(out=qtmp[r][:, :n], in_=ps[pa][:, :n], func=AF.Copy),
                                 reads=[PS(pa)], writes=[B("qtmp", r)])
                            if pending is not None:
                                rope_tail(pending)
                            pending = (j, g, t0, n, pa, pb, r, rq)
                        else:
                            def mmB(e):
                                for kc in range(16):
                                    ins = e.matmul(ps[pb][:, :n], lhsT=wsl[sbb][:, kc, j * 128:(j + 1) * 128], rhs=H[:, kc, t0:t0 + n],
                                                   start=(kc == 0), stop=(kc == 15))
                                return ins
                            P.op("pe", mmB, reads=[B("ws", sbb), B("BIG", g)], writes=[PS(pb)])
                            c = 4 * i + j
                            col0 = 2608 if g == 5 else 16 + t0
                            P.op("act", lambda e: e.activation(out=t2[r][:, :n], in_=ps[pb][:, :n], func=AF.Sigmoid),
                                 reads=[PS(pb)], writes=[B("t2", r)])
                            P.op("dve", lambda e: e.tensor_tensor(out=ot[r][:, :n], in0=ps[pa][:, :n], in1=t2[r][:, :n], op=ALU.mult),
                                 reads=[PS(pa), B("t2", r)], writes=[B("ot", r)])
                            P.op("sp", lambda e: e.dma_start(out=hcs[c, :, col0:col0 + n], in_=ot[r][:, :n]),
                                 reads=[B("ot", r)], writes=[B("hcs", g)], dma=f"ot{r}")
                if pending is not None:
                    rope_tail(pending)
                if kind == "kv":
                    for bix, blk in enumerate(list(range(16 + nAB // 128)) + [20, 21]):
                        pv_ = 6 + bix % 2

                        def mmV(e, blk=blk, pv_=pv_, sa=sa):
                            for kc in range(16):
                                ins = e.matmul(ps[pv_][:, 0:256], lhsT=H[:, kc, blk * 128:(blk + 1) * 128], rhs=wsl[sa][:, kc, 256:512],
                                               start=(kc == 0), stop=(kc == 15))
                            return ins
                        P.op("pe", mmV, reads=[B("ws", sa), B("BIG", gof(blk * 128))], writes=[PS(pv_)])
                        P.op("act", lambda e, blk=blk, pv_=pv_: e.activation(out=V[:, blk, :], in_=ps[pv_][:, 0:256], func=AF.Copy),
                             reads=[PS(pv_)], writes=[B("V", blk)])
            P.barrier()

            A = Carver(arena, S0, ARENA_BYTES)
            qg = [A.take([128, 8, 512], BF16) for _ in range(2)]
            pt = [A.take([128, 5, 512], BF16) for _ in range(3)]
            rc = [A.take([128, 512], F32) for _ in range(2)]
            if l == 0:
                feed_begin(0, 32, 52, [A.take([128, 16, 128], BF16) for _ in range(2)], 7)
            cctr = [0, 0, 0]
            pending_c = None

            def attn_tail(g, t0, bi, kvh, keys, pi):
                pvb = 3 + cctr[2] % 2
                dnb = 5 + cctr[2] % 2
                ri = cctr[2] % 2
                cctr[2] += 1
                nk = len(keys)

                def mmPV(e):
                    for idx, (kb, m) in enumerate(keys):
                        ins = e.matmul(ps[pvb][:], lhsT=V[:, kb, kvh * 128:(kvh + 1) * 128], rhs=pt[pi][:, idx, :],
                                       start=(idx == 0), stop=(idx == nk - 1))
                    return ins

                def mmDN(e):
                    for idx in range(nk):
                        e.matmul(ps[dnb][:], lhsT=ones_b[:], rhs=pt[pi][:, idx, :], start=(idx == 0), stop=False)
                    o = kvh * 512
                    return e.matmul(ps[dnb][:], lhsT=ones_b[0:1, :], rhs=esink[0:1, o:o + 512], start=False, stop=True)
                P.op("pe", mmPV, reads=[B("pt", pi)] + [B("V", kb) for kb, _ in keys], writes=[PS(pvb)])
                P.op("pe", mmDN, reads=[B("pt", pi), B("ones"), B("esink")], writes=[PS(dnb)])
                P.op("dve", lambda e: e.reciprocal(out=rc[ri][:], in_=ps[dnb][:]), reads=[PS(dnb)], writes=[B("rc", ri)])
                tb = t0 + bi * 128
                P.op("dve", lambda e: e.tensor_tensor(
                    out=H[:, 4 * kvh:4 * kvh + 4, tb:tb + 128], in0=ps[pvb][:].rearrange("p (h r) -> p h r", h=4),
                    in1=rc[ri][:].rearrange("p (h r) -> p h r", h=4), op=ALU.mult),
                    reads=[PS(pvb), B("rc", ri)], writes=[B("BIG", g)])

            for gi, (g, t0, n) in enumerate(GCD):
                qi = gi % 2
                P.op("sp", lambda e: e.dma_start(out=qg[qi][:, :, :n], in_=qs[:, :, t0:t0 + n].rearrange("h p t -> p h t")),
                     reads=[B("qs", g)], writes=[B("qg", qi)], dma=f"qg{qi}")
                for bi in range(n // 128):
                    blk = t0 // 128 + bi
                    for kvh in range(2):
                        if g == 5:
                            keys = [(20, None), (21, None)]
                        else:
                            keys = []
                            if blk - 1 >= 0:
                                keys.append((blk - 1, 0))
                            keys.append((blk, None))
                            if blk + 1 <= kb_max:
                                keys.append((blk + 1, 1))
                            keys += [(20, None), (21, None)]
                        pi = cctr[0] % 3
                        cctr[0] += 1
                        for idx, (kb, m) in enumerate(keys):
                            sbk = cctr[1] % 3
                            cctr[1] += 1

                            def mmS(e):
                                ins = e.matmul(ps[sbk][:].rearrange("p (h r) -> p h r", h=4), lhsT=KT[:, kvh, kb * 128:(kb + 1) * 128],
                                               rhs=qg[qi][:, 4 * kvh:4 * kvh + 4, bi * 128:(bi + 1) * 128], start=True, stop=(m is None))
                                if m is not None:
                                    ins = e.matmul(ps[sbk][:], lhsT=ident_b[:], rhs=mask_b[:, m, :], start=False, stop=True)
                                return ins
                            P.op("pe", mmS, reads=[B("KT", gof(kb * 128)), B("qg", qi), B("ones")], writes=[PS(sbk)])
                            P.op("act", lambda e: e.activation(out=pt[pi][:, idx, :], in_=ps[sbk][:], func=AF.Exp, scale=SCALE),
                                 reads=[PS(sbk)], writes=[B("pt", pi)])
                        if pending_c is not None:
                            attn_tail(*pending_c)
                        pending_c = (g, t0, bi, kvh, keys, pi)
                        if l == 0:
                            feed_step()
            if pending_c is not None:
                attn_tail(*pending_c)
            if l == 0:
                feed_finish()
            P.barrier()

            A = Carver(arena, S0, ARENA_BYTES)
            RT = [A.take([128, 4, 544], BF16) for _ in range(3)]
            accs = [A.take([128, 8, 512], F32) for _ in range(2)]
            cw = A.take([128, 8192], BF16)
            for q4 in range(4):
                P.op("pool", lambda e: e.dma_start(out=cw[:, q4 * 2048:(q4 + 1) * 2048],
                                                   in_=convw_d[:, l * 8192 + q4 * 2048:l * 8192 + (q4 + 1) * 2048]),
                     writes=[B("cw", q4)], dma=f"ws{q4}")
            csq = [A.take([128, 512], BF16) for _ in range(2)]
            accb = [A.take([128, 512], BF16) for _ in range(2)]
            mean = A.take([128, 512], F32)
            crs = A.take([128, 512], F32)
            cctr = [0, 0, 0]

            pend_stats = []
            for gi, (g, t0, n) in enumerate(GCD):
                col0 = 2608 if g == 5 else 16 + t0
                acc = accs[gi % 2]
                sb_, qb_ = (3, 4) if gi % 2 == 0 else (5, 6)
                for c in range(8):
                    cp = convp[:, l * 8 + c, :]
                    di = cctr[0] % 3
                    cctr[0] += 1
                    cb = cctr[1] % 3
                    cctr[1] += 1

                    def ldR(e):
                        src = hcs[c].rearrange("(cbi cc) t -> cc cbi t", cc=32)
                        for j in range(4):
                            ins = e.dma_start(out=RT[di][j * 32:(j + 1) * 32, :, 0:n + 28],
                                              in_=src[:, :, col0 - 15 + j:col0 - 15 + j + n + 28])
                        return ins
                    P.op("sp", ldR, reads=[B("hcs", gg) for gg in range(6)] + [B("hcs_pad")], writes=[B("RT", di)], dma=f"dg{di}")

                    def mmC(e):
                        for q in range(8):
                            for cbi in range(4):
                                wo_ = ((c * 8 + q) * 4 + cbi) * 32
                                ins = e.matmul(ps[cb][cbi * 32:(cbi + 1) * 32, :n], lhsT=cw[:, wo_:wo_ + 32], rhs=RT[di][:, cbi, 4 * q:4 * q + n],
                                               start=(q == 0), stop=(q == 7), tile_position=(0, cbi * 32))
                        return ins
                    P.op("pe", mmC, reads=[B("RT", di)] + [B("cw", q4) for q4 in range(4)], writes=[PS(cb)])
                    P.op("act", lambda e: e.activation(out=acc[:, c, :n], in_=ps[cb][:, :n], func=AF.Identity, bias=cp[:, 31:32]),
                         reads=[PS(cb), B("convp")], writes=[B("acc", gi % 2, c)])
                    i2 = cctr[2] % 2
                    cctr[2] += 1
                    P.op("act", lambda e: e.activation(out=csq[i2][:, :n], in_=acc[:, c, :n], func=AF.Square),
                         reads=[B("acc", gi % 2, c)], writes=[B("csq", i2)])
                    P.op("dve", lambda e: e.tensor_copy(out=accb[i2][:, :n], in_=acc[:, c, :n]),
                         reads=[B("acc", gi % 2, c)], writes=[B("accb", i2)])
                    def stats_mm(c=c, i2=i2, n=n, sb_=sb_, qb_=qb_):
                        P.op("pe", lambda e: e.matmul(ps[sb_][:, :n], lhsT=ones_b[:], rhs=accb[i2][:, :n], start=(c == 0), stop=(c == 7)),
                             reads=[B("accb", i2), B("ones")], writes=[PS(sb_)])
                        P.op("pe", lambda e: e.matmul(ps[qb_][:, :n], lhsT=ones_b[:], rhs=csq[i2][:, :n], start=(c == 0), stop=(c == 7)),
                             reads=[B("csq", i2), B("ones")], writes=[PS(qb_)])
                    pend_stats.append(stats_mm)
                    if len(pend_stats) > 1:
                        pend_stats.pop(0)()
                while pend_stats:
                    pend_stats.pop(0)()
                P.op("act", lambda e: e.activation(out=mean[:, :n], in_=ps[sb_][:, :n], func=AF.Identity, scale=1.0 / 1024),
                     reads=[PS(sb_)], writes=[B("mean")])
                P.op("dve", lambda e: e.tensor_tensor(out=crs[:, :n], in0=mean[:, :n], in1=mean[:, :n], op=ALU.mult),
                     reads=[B("mean")], writes=[B("crs")])
                P.op("dve", lambda e: e.scalar_tensor_tensor(out=crs[:, :n], in0=ps[qb_][:, :n], scalar=1.0 / 1024, in1=crs[:, :n],
                                                             op0=ALU.mult, op1=ALU.subtract),
                     reads=[PS(qb_), B("crs")], writes=[B("crs")])
                P.op("act", lambda e: e.activation(out=crs[:, :n], in_=crs[:, :n], func=AF.Sqrt, bias=eps_t[:]), reads=[B("crs")], writes=[B("crs")])
                P.op("dve", lambda e: e.reciprocal(out=crs[:, :n], in_=crs[:, :n]), reads=[B("crs")], writes=[B("crs")])
                for c in range(8):
                    cp = convp[:, l * 8 + c, :]
                    P.op("dve", lambda e: e.tensor_tensor(out=acc[:, c, :n], in0=acc[:, c, :n], in1=mean[:, :n], op=ALU.subtract),
                         reads=[B("acc", gi % 2, c), B("mean")], writes=[B("acc", gi % 2, c)])
                    P.op("dve", lambda e: e.tensor_tensor(out=acc[:, c, :n], in0=acc[:, c, :n], in1=crs[:, :n], op=ALU.mult),
                         reads=[B("acc", gi % 2, c), B("crs")], writes=[B("acc", gi % 2, c)])
                    P.op("act", lambda e: e.activation(out=H[:, 8 + c, t0:t0 + n], in_=acc[:, c, :n], func=AF.Silu,
                                                       scale=cp[:, 32:33], bias=cp[:, 33:34]),
                         reads=[B("acc", gi % 2, c), B("convp")], writes=[B("BIG", g)])
            P.barrier()

            A = Carver(arena, S0, ARENA_BYTES)
            wsl = [A.take([128, 16, 512], BF16) for _ in range(2)]
            xt = [A.take([128, 512], F32) for _ in range(8)]
            wo = wview(w_out[l])
            NPB = 8
            if l == 0:
                NPB = 7
                feed_begin(0, 52, 96, [A.take([128, 16, 128], BF16) for _ in range(2)], 7)

            def d_load(s):
                i = s % 2
                P.op("pool", lambda e: e.dma_start(out=wsl[i], in_=wo[:, :, s * 512:(s + 1) * 512]), writes=[B("ws", i)], dma=f"ws{i}")
            d_load(0)
            units = [(s_, j, g, t0, n) for s_ in range(4) for j in range(4) for (g, t0, n) in GCD]

            def x_load(u):
                s_, j, g, t0, n = units[u]
                dc = 4 * s_ + j
                xi = u % 8
                P.op("sp", lambda e: e.dma_start(out=xt[xi][:, :n], in_=xs[dc, :, t0:t0 + n]),
                     reads=[B("xs", dc)], writes=[B("xt", xi)], dma=f"xt{xi}")
            for u0 in range(4):
                x_load(u0)
            for u, (s_, j, g, t0, n) in enumerate(units):
                if j == 0 and g == 0 and s_ + 1 < 4:
                    d_load(s_ + 1)
                if u + 4 < len(units):
                    x_load(u + 4)
                i = s_ % 2
                dc = 4 * s_ + j
                v = vmod(g)
                xi = u % 8
                pb = u % NPB
                if l == 0:
                    feed_step()

                def mmO(e):
                    for kc in range(16):
                        ins = e.matmul(ps[pb][:, :n], lhsT=wsl[i][:, kc, j * 128:(j + 1) * 128], rhs=H[:, kc, t0:t0 + n],
                                       start=(kc == 0), stop=(kc == 15))
                    return ins
                P.op("pe", mmO, reads=[B("ws", i), B("BIG", g)], writes=[PS(pb)])
                P.op("dve", lambda e: e.scalar_tensor_tensor(
                    out=xt[xi][:, :n], in0=ps[pb][:, :n], scalar=mod[:, 32 + dc, v:v + 1], in1=xt[xi][:, :n], op0=ALU.mult, op1=ALU.add),
                    reads=[PS(pb), B("xt", xi), bmod], writes=[B("xt", xi)])
                P.op("sp", lambda e: e.dma_start(out=xs[dc, :, t0:t0 + n], in_=xt[xi][:, :n]),
                     reads=[B("xt", xi)], writes=[B("xsD", dc, g)], dma=f"xo{xi}")
            if l == 0:
                feed_finish()
                ada_gmod(0, parts=(1,))
            P.barrier()

            XACC = Carver(arena, 0, 65536).take([128, 16, 1024], F32)
            H2 = Carver(arena, 65536, 98304).take([128, 16, 1024], BF16)
            A = Carver(arena, 98304, ARENA_BYTES)
            W1 = [A.take([128, 16, 512], BF16) for _ in range(2)]
            W2 = [A.take([128, 4, 2048], BF16) for _ in range(2)]
            hid = [A.take([128, 4, 512], BF16) for _ in range(2)]
            rt = [A.take([128, 512], BF16) for _ in range(2)]
            sq = [A.take([128, 512], BF16) for _ in range(2)]
            rstd = A.take([128, 1024], F32)
            tmpn = A.take([128, 1024], F32)
            tail = Carver(arena, A.off, ARENA_BYTES)
            if last:
                yt = [tail.take([128, 512], F32) for _ in range(2)]
            else:
                wa = [tail.take([128, 16, 128], BF16) for _ in range(2)]
                wav = wview(w_ada[l + 1])
            w1v = wview(w1_d[l])
            w2v = w2_d[l].rearrange("(hs p) c -> p hs c", p=128)
            mctr = [0, 0, 0, 0, 0]
            def mk_subs(ranges):
                subs_, o_ = [], 0
                for (lo, hi, v_) in ranges:
                    m_ = hi - lo
                    parts = [m_] if m_ <= 512 else [m_ // 2, m_ - m_ // 2]
                    c_ = lo
                    for p_ in parts:
                        subs_.append((o_, p_, v_, c_))
                        o_ += p_
                        c_ += p_
                assert o_ <= 1024
                return subs_
            if last:
                sglist = [(0, mk_subs([(0, 1024, 0)])), (1024, mk_subs([(1024, 2048, 0)]))]
            else:
                lat = 2048 + nCD
                a_ = min(1024, -(-(lat + 256) // 3 // 64) * 64)
                sglist = [(0, mk_subs([(0, a_, 0)])), (a_, mk_subs([(a_, 2 * a_, 0)])),
                          (2 * a_, mk_subs([(2 * a_, lat, 0), (2560, 2816, 1)]))]
                assert sum(len(sg[1]) for sg in sglist) * 16 >= 96
                assert len(sglist[0][1]) == 2 and len(sglist[1][1]) == 2
            sbank = [6, 7, 5]

            def wa_load(oc):
                i = oc % 2
                P.op("pool", lambda e: e.dma_start(out=wa[i], in_=wav[:, :, oc * 128:(oc + 1) * 128]), writes=[B("wa", i)], dma=f"wa{i}")
            unit = 0
            if not last:
                wa_load(0)
            preloaded = [False]

            def emit_ld(c, subs_):
                def ld(e):
                    for (o, n, v, col) in subs_:
                        ins = e.dma_start(out=XACC[:, c, o:o + n], in_=xs[c, :, col:col + n])
                    return ins
                P.op("sp", ld, reads=[B("xs", c)], writes=[B("XACC", c)], dma=f"xacc{c}")

            for sgi, (s0, subs) in enumerate(sglist):

                def m_load(hsg):
                    i = hsg % 2
                    P.op("pool", lambda e: e.dma_start(out=W1[i], in_=w1v[:, :, hsg * 512:(hsg + 1) * 512]), writes=[B("W1", i)], dma=f"W1{i}")
                    P.op("pool", lambda e: e.dma_start(out=W2[i], in_=w2v[:, hsg * 4:(hsg + 1) * 4, :]), writes=[B("W2", i)], dma=f"W2{i}")
                m_load(0)

                def stats(final):
                    for c in range(16):
                        if not final and not preloaded[0]:
                            emit_ld(c, subs)
                        for k, (o, n, v, col) in enumerate(subs):
                            i2 = mctr[0] % 2
                            mctr[0] += 1
                            P.op("act", lambda e: e.activation(out=sq[i2][:, :n], in_=XACC[:, c, o:o + n], func=AF.Square),
                                 reads=[B("XACC", c)], writes=[B("sq", i2)])
                            P.op("pe", lambda e: e.matmul(ps[sbank[k]][:, :n], lhsT=ones_b[:], rhs=sq[i2][:, :n], start=(c == 0), stop=(c == 15)),
                                 reads=[B("sq", i2), B("ones")], writes=[PS(sbank[k])])
                    for k, (o, n, v, col) in enumerate(subs):
                        P.op("act", lambda e: e.activation(out=rstd[:, o:o + n], in_=ps[sbank[k]][:, :n], func=AF.Sqrt, scale=1.0 / D, bias=eps_t[:]),
                             reads=[PS(sbank[k])], writes=[B("rstd")])
                        P.op("dve", lambda e: e.reciprocal(out=rstd[:, o:o + n], in_=rstd[:, o:o + n]), reads=[B("rstd")], writes=[B("rstd")])

                stats(False)
                preloaded[0] = False
                for c in range(16):
                    for k, (o, n, v, col) in enumerate(subs):
                        P.op("dve", lambda e: e.tensor_tensor(out=tmpn[:, o:o + n], in0=XACC[:, c, o:o + n], in1=rstd[:, o:o + n], op=ALU.mult),
                             reads=[B("XACC", c), B("rstd")], writes=[B("tmpn", k)])
                        P.op("act", lambda e: e.activation(out=H2[:, c, o:o + n], in_=tmpn[:, o:o + n], func=AF.Identity,
                                                           scale=gmod[:, 1, c, v:v + 1], bias=mod[:, 48 + c, v:v + 1]),
                             reads=[B("tmpn", k), bgmod, bmod], writes=[B("H2", k)])
                def stage2(hsg, k, hi):
                    o, n, v, col = subs[k]
                    wi_ = hsg % 2
                    for dc in range(16):
                        yb = 3 + mctr[4] % 3
                        mctr[4] += 1

                        def mm2(e):
                            for j in range(4):
                                ins = e.matmul(ps[yb][:, :n], lhsT=W2[wi_][:, j, dc * 128:(dc + 1) * 128], rhs=hid[hi][:, j, :n],
                                               start=(j == 0), stop=(j == 3))
                            return ins
                        P.op("pe", mm2, reads=[B("W2", wi_), B("hid", hi)], writes=[PS(yb)])
                        P.op("dve", lambda e: e.scalar_tensor_tensor(
                            out=XACC[:, dc, o:o + n], in0=ps[yb][:, :n], scalar=mod[:, 80 + dc, v:v + 1], in1=XACC[:, dc, o:o + n],
                            op0=ALU.mult, op1=ALU.add),
                            reads=[PS(yb), B("XACC", dc), bmod], writes=[B("XACC", dc)])

                pend2 = None
                for hsg in range(16):
                    wi_ = hsg % 2
                    for k, (o, n, v, col) in enumerate(subs):
                        if not last and unit < 96:
                            oc = unit
                            unit += 1
                            if oc + 1 < 96:
                                wa_load(oc + 1)
                            ada_mm(l + 1, oc, wa[oc % 2], 0, B("wa", oc % 2), 7)
                            if oc % 4 == 3:
                                ada_evac(l + 1, oc // 4, 7)
                        hi = mctr[1] % 2
                        mctr[1] += 1
                        for j in range(4):
                            hb = mctr[2] % 3
                            mctr[2] += 1
                            ri = mctr[3] % 2
                            mctr[3] += 1

                            def mm1(e):
                                for kc in range(16):
                                    ins = e.matmul(ps[hb][:, :n], lhsT=W1[wi_][:, kc, j * 128:(j + 1) * 128], rhs=H2[:, kc, o:o + n],
                                                   start=(kc == 0), stop=(kc == 15))
                                return ins
                            P.op("pe", mm1, reads=[B("W1", wi_), B("H2", k)], writes=[PS(hb)])
                            P.op("act", lambda e: e.activation(out=rt[ri][:, :n], in_=ps[hb][:, :n], func=AF.Relu),
                                 reads=[PS(hb)], writes=[B("rt", ri)])
                            P.op("dve", lambda e: e.tensor_tensor(out=hid[hi][:, j, :n], in0=ps[hb][:, :n], in1=rt[ri][:, :n], op=ALU.mult),
                                 reads=[PS(hb), B("rt", ri)], writes=[B("hid", hi)])
                        if pend2 is not None:
                            stage2(*pend2)
                        pend2 = (hsg, k, hi)
                        if k == 0 and hsg + 1 < 16:
                            m_load(hsg + 1)
                if pend2 is not None:
                    stage2(*pend2)
                if not last:
                    nxt = sglist[sgi + 1][1] if sgi + 1 < len(sglist) else None
                    LAG = 8
                    for c in range(16):
                        def stx(e):
                            for (o, n, v, col) in subs:
                                ins = e.dma_start(out=xs[c, :, col:col + n], in_=XACC[:, c, o:o + n])
                            return ins
                        P.op("sp", stx, reads=[B("XACC", c)], writes=[B("xs", c)], dma=f"xst{c}")
                        if nxt is not None and c >= LAG:
                            emit_ld(c - LAG, nxt)
                    if nxt is not None:
                        for c in range(16 - LAG, 16):
                            emit_ld(c, nxt)
                        preloaded[0] = True
                else:
                    stats(True)
                    for c in range(16):
                        for k, (o, n, v, col) in enumerate(subs):
                            yi = mctr[0] % 2
                            mctr[0] += 1
                            P.op("dve", lambda e: e.tensor_tensor(out=tmpn[:, o:o + n], in0=XACC[:, c, o:o + n], in1=rstd[:, o:o + n], op=ALU.mult),
                                 reads=[B("XACC", c), B("rstd")], writes=[B("tmpn", k)])
                            P.op("act", lambda e: e.activation(out=yt[yi][:, :n], in_=tmpn[:, o:o + n], func=AF.Identity,
                                                               scale=gvec[:, 2 * L, c:c + 1]),
                                 reads=[B("tmpn", k), B("gvec")], writes=[B("yt", yi)])
                            P.op("sp", lambda e: e.dma_start(out=y_d[c, :, col:col + n], in_=yt[yi][:, :n]),
                                 reads=[B("yt", yi)], writes=[B("yout")], dma=f"yo{yi}")
            if not last:
                assert unit == 96
                ada_gmod(l + 1)
            P.barrier()
        P.wait_all("sp", [B("yout")])
        P.emit()
    return nc


def _rope_tables(pos):
    pos = np.asarray(pos, dtype=np.int64)
    row = (pos // 64).astype(np.float32)
    col = (pos % 64).astype(np.float32)
    n_freq = 32
    inv = (np.float32(10000.0) ** (-np.arange(n_freq, dtype=np.float32) / np.float32(n_freq))).astype(np.float32)
    theta = np.concatenate([row[:, None] * inv, col[:, None] * inv], axis=-1)
    theta = np.concatenate([theta, theta], axis=-1)
    return np.cos(theta).astype(np.float32), np.sin(theta).astype(np.float32)


def _fm(a2d):
    t = a2d.shape[0]
    return np.ascontiguousarray(a2d.T.reshape(16, 128, t))


def _vec_fm(v, nch):
    v = np.asarray(v)
    lead = v.shape[:-1]
    r = v.reshape(lead + (nch, 128))
    return np.ascontiguousarray(np.moveaxis(r, -1, 0))


def prepare_shared(inp, L):
    f = lambda a: np.ascontiguousarray(np.asarray(a, dtype=np.float32))
    sh = {}
    w_in = np.asarray(inp["w_in"])[:L]
    sh["w_in"] = f(w_in)
    sh["w_ada"] = f(np.asarray(inp["w_ada"])[:L])
    b = _vec_fm(np.asarray(inp["b_ada"])[:L], 96)
    sh["b_ada2"] = np.ascontiguousarray(np.repeat(b[:, :, :, None], 2, axis=3).reshape(128, L * 96 * 2))
    gv = np.concatenate([np.asarray(inp["g_mix"])[:L], np.asarray(inp["g_mlp"])[:L], np.asarray(inp["g_final"])[None]], axis=0)
    sh["gvec"] = np.ascontiguousarray(_vec_fm(gv, 16).reshape(128, (2 * L + 1) * 16))
    sink = np.asarray(inp["attn_sink"])[:L].reshape(L, 2, 4)
    sh["sinkrow"] = np.ascontiguousarray(np.repeat(sink[:, :, :, None], 128, axis=3).reshape(1, L * 2 * 512)).astype(np.float32)
    sh["w_out"] = f(np.asarray(inp["w_out"])[:L])
    sh["w1"] = f(np.asarray(inp["w_mlp1"])[:L])
    sh["w2"] = f(np.asarray(inp["w_mlp2"])[:L])
    ident = np.eye(128, dtype=np.float32)
    j = np.arange(128)[:, None]
    r = np.arange(128)[None, :]
    mL = np.where(j >= r, 0.0, -30000.0).astype(np.float32)
    mR = np.where(j <= r, 0.0, -30000.0).astype(np.float32)
    sh["cst"] = np.ascontiguousarray(np.concatenate([ident, np.tile(mL, (1, 4)), np.tile(mR, (1, 4))], axis=1))
    cw = np.asarray(inp["conv_w"])[:L]
    sh["_conv"] = (cw, np.asarray(inp["conv_b"])[:L], np.asarray(inp["conv_ln_g"])[:L], np.asarray(inp["conv_ln_b"])[:L])
    return sh


def prepare_core(inp, sh, core, L):
    b, half = core // 2, core % 2
    x = np.asarray(inp["x"])[b]
    if half == 0:
        pos = np.arange(0, NLAT)
    else:
        pos = np.arange(4095, 4095 - NLAT, -1)
    xl = x[pos]
    ctx = np.asarray(inp["ctx"])[b]
    m = {k: v for k, v in sh.items() if not k.startswith("_")}
    m["xin"] = _fm(np.concatenate([xl, ctx], axis=0))
    cc = np.stack([np.asarray(inp["c"])[b], np.asarray(inp["c_ctx"])], axis=-1)
    m["cT"] = np.ascontiguousarray(cc.reshape(16, 128, 2).transpose(1, 0, 2)).astype(np.float32)
    cos, sin = _rope_tables(pos)
    sgn = np.concatenate([-np.ones(64, np.float32), np.ones(64, np.float32)])
    cosT = np.concatenate([cos.T, np.ones((128, NCTX), np.float32)], axis=1)
    sinT = np.concatenate([(sin * sgn[None, :]).T, np.zeros((128, NCTX), np.float32)], axis=1)
    m["rope"] = np.ascontiguousarray(np.stack([cosT, sinT], axis=1)).astype(np.float32)
    cw, cb, lg, lb = sh["_conv"]
    if half == 1:
        cw = cw[:, ::-1, :]
    taps = np.moveaxis(cw.reshape(L, 31, 8, 128), (3, 0, 2, 1), (0, 1, 2, 3))
    extra = np.stack([_vec_fm(cb, 8), _vec_fm(lg, 8), _vec_fm(lb, 8)], axis=-1)
    m["convp"] = np.ascontiguousarray(np.concatenate([taps, extra], axis=-1).reshape(128, L * 8 * 34)).astype(np.float32)
    wp = np.concatenate([cw, np.zeros((L, 1, 1024), cw.dtype)], axis=1).reshape(L, 8, 4, 8, 4, 32)
    T = np.zeros((4, 32, L, 8, 8, 4, 32), np.float32)
    for c_ in range(32):
        T[:, c_, :, :, :, :, c_] = wp[..., c_].transpose(2, 0, 3, 1, 4)
    m["convw"] = np.ascontiguousarray(T.reshape(128, L * 8192))
    return m


def assemble(results, n_cores_run, cores):
    out = np.zeros((4, 4096, D), np.float32)
    for core, res in zip(cores, results):
        b, half = core // 2, core % 2
        yt = res["y"].reshape(D, NOWN).T
        if half == 0:
            out[b, 0:NOWN] = yt
        else:
            out[b, 4095 - np.arange(NOWN)] = yt
    return out


_CACHE = {}


def kernel(**inputs):
    L = 4
    if L not in _CACHE:
        _CACHE[L] = build_program(L)
    nc = _CACHE[L]
    sh = prepare_shared(inputs, L)
    cores = list(range(8))
    in_maps = [prepare_core(inputs, sh, c, L) for c in cores]
    res = run_bass_kernel_spmd(nc, in_maps, core_ids=cores)
    return assemble(res.results, 8, cores)
```

```python
import contextlib
import numpy as np
import concourse.bass as bass
import concourse.mybir as mybir
from concourse.bass_utils import run_bass_kernel_spmd

F32 = mybir.dt.float32
BF16 = mybir.dt.bfloat16
AF = mybir.ActivationFunctionType
ALU = mybir.AluOpType

D = 2048
NLAT = 2560
NCTX = 256
NT = NLAT + NCTX
NOWN = 2048
HCW = 2880
GROUPS = [(0, 512), (512, 512), (1024, 512), (1536, 512), (2048, 512), (2560, 256)]
SGS = [(0, 1024, [0, 1]), (1024, 1024, [2, 3]), (2048, 768, [4, 5])]
EPS = 1e-6
SCALE = 128.0 ** -0.5


class Buf:
    __slots__ = ("name", "w", "r")

    def __init__(self, name):
        self.name = name
        self.w = None
        self.r = {}


class Rec:
    def __init__(self):
        self.calls = []

    def __getattr__(self, name):
        def f(*a, **k):
            self.calls.append((name, a, k))
            return self
        return f


class Prog:
    ENGS = ("pe", "act", "dve", "pool", "sp")

    def __init__(self, nc, stack):
        self.nc = nc
        self.stack = stack
        self.ops = {e: [] for e in self.ENGS}
        self.cnt = {}
        self.sems = {}
        self.seen = {e: {} for e in self.ENGS}
        self.bufs = {}
        for e in self.ENGS:
            self._sem(e)

    def B(self, *key):
        b = self.bufs.get(key)
        if b is None:
            b = self.bufs[key] = Buf(key)
        return b

    def _sem(self, key):
        if key not in self.sems:
            self.sems[key] = self.stack.enter_context(self.nc.semaphore("s_" + key))
            self.cnt[key] = 0
        return self.sems[key]

    def op(self, eng, fn, reads=(), writes=(), dma=None):
        deps = {}
        for b in reads:
            if b.w is not None and deps.get(b.w[0], 0) < b.w[1]:
                deps[b.w[0]] = b.w[1]
            if b.name[0] == "ps":
                for k, v in b.r.items():
                    if k != eng and deps.get(k, 0) < v:
                        deps[k] = v
        for b in writes:
            if b.w is not None and deps.get(b.w[0], 0) < b.w[1]:
                deps[b.w[0]] = b.w[1]
            for k, v in b.r.items():
                if deps.get(k, 0) < v:
                    deps[k] = v
        if dma is None:
            key = eng
            inc = 1
        else:
            key = "d_" + dma
            self._sem(key)
            inc = 16
        self.cnt[key] += inc
        tick = (key, self.cnt[key])
        waits = []
        for k, v in deps.items():
            if k == eng:
                if eng == "pe":
                    continue
                if dma is None and self.cnt[eng] - v > 3:
                    continue
            if self.seen[eng].get(k, 0) >= v:
                continue
            self.seen[eng][k] = v
            waits.append((k, v))
        for b in reads:
            if b.r.get(tick[0], 0) < tick[1]:
                b.r[tick[0]] = tick[1]
        for b in writes:
            b.w = tick
            b.r = {}
        rec = Rec()
        fn(rec)
        if dma is not None and len(rec.calls) > 1:
            extra = 16 * (len(rec.calls) - 1)
            self.cnt[key] += extra
            tick = (key, self.cnt[key])
            for b in reads:
                b.r[tick[0]] = tick[1]
            for b in writes:
                b.w = tick
        self.ops[eng].append((waits, rec.calls, key, inc))

    def wait_all(self, eng, bufs):
        waits = []
        for b in bufs:
            items = list(b.r.items())
            if b.w is not None:
                items.append(b.w)
            for k, v in items:
                if self.seen[eng].get(k, 0) >= v:
                    continue
                self.seen[eng][k] = v
                waits.append((k, v))
        self.ops[eng].append((waits, None, None, 0))

    def barrier(self):
        snap = {k: v for k, v in self.cnt.items() if v > 0 and k != "sp"}
        for e in self.ENGS:
            waits = []
            for k, v in snap.items():
                if k == e:
                    continue
                if self.seen[e].get(k, 0) >= v:
                    continue
                self.seen[e][k] = v
                waits.append((k, v))
            if waits:
                self.ops[e].append((waits, None, None, 0))

    def emit(self):
        nc = self.nc
        with nc.Block() as block:
            def run(name):
                def body(e):
                    for waits, fn, key, inc in self.ops[name]:
                        for k, v in waits:
                            e.wait_ge(self.sems[k], v)
                        if fn is not None:
                            for name_, a, k in fn:
                                ins = getattr(e, name_)(*a, **k)
                                if inc == 16:
                                    ins.then_inc(self.sems[key], 16)
                            if inc != 16:
                                ins.then_inc(self.sems[key], inc)
                return body
            block.tensor(run("pe"))
            block.scalar(run("act"))
            block.vector(run("dve"))
            block.gpsimd(run("pool"))
            block.sync(run("sp"))


class Carver:
    def __init__(self, arena, base, limit):
        self.arena = arena
        self.off = base
        self.limit = limit

    def take(self, shape, dt):
        n = 1
        for s in shape[1:]:
            n *= s
        nb = n * (2 if dt == BF16 else 4)
        nb = (nb + 31) // 32 * 32
        assert self.off + nb <= self.limit, (self.off, nb, self.limit)
        v = self.arena[:, self.off // 2:(self.off + nb) // 2]
        if dt == F32:
            v = v.bitcast(F32)
        v = v[:, 0:n]
        if len(shape) == 3:
            v = v.rearrange("p (a b) -> p a b", a=shape[1])
        self.off += nb
        return v


ARENA_BYTES = 194816
S0 = 112640


def build_program(L):
    nc = bass.Bass("TRN2", target_bir_lowering=False)

    def din(name, shape):
        return nc.dram_tensor(name, shape, F32, kind="ExternalInput").ap()

    xin = din("xin", [16, 128, NT])
    cT_d = din("cT", [128, 16, 2])
    rope_d = din("rope", [128, 2, NT])
    cst_d = din("cst", [128, 128 + 1024])
    w_ada = din("w_ada", [L, D, 6 * D])
    bada_d = din("b_ada2", [128, L * 96 * 2])
    gvec_d = din("gvec", [128, (2 * L + 1) * 16])
    w_in = din("w_in", [L, D, 3584])
    sink_d = din("sinkrow", [1, L * 2 * 512])
    convp_d = din("convp", [128, L * 8 * 34])
    convw_d = din("convw", [128, L * 8192])
    w_out = din("w_out", [L, D, D])
    w1_d = din("w1", [L, D, 4 * D])
    w2_d = din("w2", [L, 4 * D, D])
    y_d = nc.dram_tensor("y", [16, 128, NOWN], F32, kind="ExternalOutput").ap()
    xs = nc.dram_tensor("xs", [16, 128, NT], F32).ap()
    qs = nc.dram_tensor("qs", [8, 128, NT], BF16).ap()
    hcs = nc.dram_tensor("hcs", [8, 128, HCW], BF16).ap()
    dgs = nc.dram_tensor("dgs", [L * 8, 128, 31 * 128], BF16).ap()

    with contextlib.ExitStack() as st:
        P = Prog(nc, st)
        B = P.B

        def sb(name, shape, dt):
            return st.enter_context(nc.sbuf_tensor(name, shape, dt))

        arena = sb("arena", [128, ARENA_BYTES // 2], BF16)
        ones_f = sb("ones_f", [128, 128], F32)
        ones_b = sb("ones_b", [128, 128], BF16)
        ident_b = sb("ident_b", [128, 128], BF16)
        perm_b = sb("perm_b", [128, 128], BF16)
        mask_b = sb("mask_b", [128, 2, 512], BF16)
        eps_t = sb("eps_t", [128, 1], F32)
        silu_c = sb("silu_c", [128, 16, 2], BF16)
        mods = [sb(f"mod{i}", [128, 96, 2], F32) for i in range(2)]
        gmods = [sb(f"gmod{i}", [128, 2, 16, 2], F32) for i in range(2)]
        badat = sb("badat", [128, L * 96, 2], F32)
        gvec = sb("gvec_t", [128, 2 * L + 1, 16], F32)
        esink = sb("esink", [1, 2 * 512], BF16)
        convp = sb("convp_t", [128, L * 8, 34], F32)
        ps = [st.enter_context(nc.psum_tensor(f"ps{i}", [128, 512], F32)) for i in range(8)]

        def PS(i):
            return B("ps", i)

        H = Carver(arena, 0, 90112).take([128, 16, NT], BF16)
        KT = Carver(arena, 90112, 101376).take([128, 2, NT], BF16)
        V = Carver(arena, 101376, 112640).take([128, 22, 256], BF16)

        def gof(t):
            return 5 if t >= NLAT else t // 512

        def vmod(g):
            return 1 if g == 5 else 0

        A = Carver(arena, S0, ARENA_BYTES)
        cst_f = A.take([128, 1152], F32)
        cT_f = A.take([128, 32], F32)
        zt = A.take([128, 8, 32], BF16)
        P.op("sp", lambda e: e.dma_start(out=cst_f, in_=cst_d), writes=[B("cst_f")], dma="p0")
        P.op("sp", lambda e: e.dma_start(out=cT_f, in_=cT_d.rearrange("p a b -> p (a b)")), writes=[B("cT_f")], dma="p1")
        P.op("sp", lambda e: e.dma_start(out=badat[:].rearrange("p a b -> p (a b)"), in_=bada_d), writes=[B("bada")], dma="p3")
        P.op("sp", lambda e: e.dma_start(out=gvec[:].rearrange("p a b -> p (a b)"), in_=gvec_d), writes=[B("gvec")], dma="p4")
        P.op("sp", lambda e: e.dma_start(out=convp[:].rearrange("p a b -> p (a b)"), in_=convp_d), writes=[B("convp")], dma="p5")
        for c in range(16):
            P.op("sp", lambda e, c=c: e.dma_start(out=xs[c], in_=xin[c]), writes=[B("xs", c)], dma="p6")
        P.op("dve", lambda e: e.memset(ones_f[:], 1.0), writes=[B("ones")])
        P.op("dve", lambda e: e.memset(ones_b[:], 1.0), writes=[B("ones")])
        P.op("dve", lambda e: e.memset(eps_t[:], EPS), writes=[B("ones")])
        P.op("dve", lambda e: e.memset(zt, 0.0), writes=[B("zt")])
        P.op("dve", lambda e: e.tensor_copy(out=ident_b[:], in_=cst_f[:, 0:128]), reads=[B("cst_f")], writes=[B("ones")])
        P.op("dve", lambda e: e.tensor_copy(out=perm_b[:, 0:64], in_=cst_f[:, 64:128]), reads=[B("cst_f")], writes=[B("ones")])
        P.op("dve", lambda e: e.tensor_copy(out=perm_b[:, 64:128], in_=cst_f[:, 0:64]), reads=[B("cst_f")], writes=[B("ones")])
        P.op("dve", lambda e: e.tensor_copy(out=mask_b[:].rearrange("p a b -> p (a b)"), in_=cst_f[:, 128:1152]),
             reads=[B("cst_f")], writes=[B("ones")])
        P.op("act", lambda e: e.activation(out=silu_c[:].rearrange("p a b -> p (a b)"), in_=cT_f, func=AF.Silu),
             reads=[B("cT_f")], writes=[B("siluc")])
        hv = hcs.rearrange("c p t -> p c t")
        P.op("sp", lambda e: e.dma_start(out=hv[:, :, 0:16], in_=zt[:, :, 0:16]), reads=[B("zt")], writes=[B("hcs_pad")], dma="p7")
        P.op("sp", lambda e: e.dma_start(out=hv[:, :, 2576:2608], in_=zt[:, :, 0:32]), reads=[B("zt")], writes=[B("hcs_pad")], dma="p7")
        P.op("sp", lambda e: e.dma_start(out=hv[:, :, 2864:2880], in_=zt[:, :, 0:16]), reads=[B("zt")], writes=[B("hcs_pad")], dma="p7")
        P.barrier()

        def wview(w2d):
            return w2d.rearrange("(kc p) c -> p kc c", p=128)

        def ada_mm(lx, oc, wt, col, wbuf, bank):
            def mm(e):
                for kc in range(16):
                    ins = e.matmul(ps[bank][:, 2 * (oc % 4):2 * (oc % 4) + 2], lhsT=wt[:, kc, col:col + 128],
                                   rhs=silu_c[:, kc, :], start=(kc == 0), stop=(kc == 15))
                return ins
            P.op("pe", mm, reads=[wbuf, B("siluc")], writes=[PS(bank)])

        def ada_evac(lx, s4, bank):
            P.op("dve", lambda e: e.tensor_tensor(
                out=mods[lx % 2][:, 4 * s4:4 * s4 + 4, :], in0=ps[bank][:, 0:8].rearrange("p (j v) -> p j v", v=2),
                in1=badat[:, lx * 96 + 4 * s4:lx * 96 + 4 * s4 + 4, :], op=ALU.add),
                reads=[PS(bank), B("bada")], writes=[B("mod", lx % 2)])

        def ada_gmod(lx, parts=(0, 1)):
            m_, g_ = mods[lx % 2], gmods[lx % 2]
            for v in range(2):
                if 0 in parts:
                    P.op("dve", lambda e: e.scalar_tensor_tensor(out=g_[:, 0, :, v], in0=m_[:, 16:32, v], scalar=1.0,
                                                                 in1=gvec[:, lx, :], op0=ALU.add, op1=ALU.mult),
                         reads=[B("mod", lx % 2), B("gvec")], writes=[B("gmod", lx % 2)])
                if 1 in parts:
                    P.op("dve", lambda e: e.scalar_tensor_tensor(out=g_[:, 1, :, v], in0=m_[:, 64:80, v], scalar=1.0,
                                                                 in1=gvec[:, L + lx, :], op0=ALU.add, op1=ALU.mult),
                         reads=[B("mod", lx % 2), B("gvec")], writes=[B("gmod", lx % 2)])

        feed = {"oc": 96, "end": 96}

        def feed_load(oc):
            i = oc % 2
            P.op("pool", lambda e: e.dma_start(out=feed["wa"][i], in_=feed["wav"][:, :, oc * 128:(oc + 1) * 128]),
                 writes=[B("wa", i)], dma=f"wa{i}")

        def feed_begin(lx, oc0, oc1, wa_tiles, bank):
            feed.update(lx=lx, oc=oc0, end=oc1, wa=wa_tiles, bank=bank, wav=wview(w_ada[lx]))
            feed_load(oc0)

        def feed_step():
            if feed["oc"] >= feed["end"]:
                return
            oc = feed["oc"]
            feed["oc"] += 1
            if oc + 1 < feed["end"]:
                feed_load(oc + 1)
            ada_mm(feed["lx"], oc, feed["wa"][oc % 2], 0, B("wa", oc % 2), feed["bank"])
            if oc % 4 == 3:
                ada_evac(feed["lx"], oc // 4, feed["bank"])

        def feed_finish():
            while feed["oc"] < feed["end"]:
                feed_step()

        def do_esink(lx, sink_f):
            P.op("sp", lambda e: e.dma_start(out=sink_f[0:1, :], in_=sink_d[:, lx * 1024:(lx + 1) * 1024]), writes=[B("sink_f")], dma="p2")
            P.op("act", lambda e: e.activation(out=esink[:], in_=sink_f[0:1, :], func=AF.Exp), reads=[B("sink_f")], writes=[B("esink")])

        for l in range(L):
            last = l == L - 1
            rr = L - 1 - l
            nAB = 128 * (rr + 1)
            nCD = 128 * rr
            GAB = [(g, GROUPS[g][0], GROUPS[g][1]) for g in range(4)] + [(4, 2048, nAB), (5, 2560, 256)]
            GCD = [(g, GROUPS[g][0], GROUPS[g][1]) for g in range(4)]
            if nCD > 0:
                GCD.append((4, 2048, nCD))
            if not last:
                GCD.append((5, 2560, 256))
            kb_max = 15 + nAB // 128
            mod = mods[l % 2]
            gmod = gmods[l % 2]
            bmod = B("mod", l % 2)
            bgmod = B("gmod", l % 2)
            if l == 0:
                A = Carver(arena, S0, ARENA_BYTES)
                wsl = [A.take([128, 16, 512], BF16) for _ in range(3)]
                sink_f = A.take([128, 1024], F32)
                dgt = [A.take([128, 31 * 128], BF16) for _ in range(2)]
                do_esink(0, sink_f)
                wv = wview(w_ada[0])

                def ada_load(s_):
                    i = s_ % 3
                    P.op("pool", lambda e: e.dma_start(out=wsl[i], in_=wv[:, :, s_ * 512:(s_ + 1) * 512]),
                         writes=[B("ws", i)], dma=f"ws{i}")

                def mkdiag_set(lc):
                    di = lc % 2

                    def mkdiag(e):
                        for k in range(31):
                            ins = e.tensor_scalar_mul(out=dgt[di][:, k * 128:(k + 1) * 128], in0=ident_b[:], scalar1=convp[:, lc, k:k + 1])
                        return ins
                    P.op("dve", mkdiag, reads=[B("ones"), B("convp")], writes=[B("dgt", di)])
                    P.op("sp", lambda e: e.dma_start(out=dgs[lc], in_=dgt[di]), reads=[B("dgt", di)], writes=[B("dgs")], dma=f"dgt{di}")

                NSL = 8
                ada_load(0)
                ada_load(1)
                nsets = 0
                for s_ in range(NSL):
                    if s_ + 2 < NSL:
                        ada_load(s_ + 2)
                    i = s_ % 3
                    bank = s_ % 2
                    for j in range(4):
                        ada_mm(0, 4 * s_ + j, wsl[i], j * 128, B("ws", i), bank)
                    ada_evac(0, s_, bank)
                ada_gmod(0, parts=(0,))
                P.barrier()

            A = Carver(arena, S0, ARENA_BYTES)
            xg = [A.take([128, 16, 512], F32) for _ in range(2)]
            sqb = [A.take([128, 512], BF16) for _ in range(4)]
            tmp = [A.take([128, 512], F32) for _ in range(2)]
            rstd = A.take([128, 512], F32)
            if l > 0:
                do_esink(l, A.take([128, 1024], F32))
            ctr = [0, 0]
            for g, t0, n in GAB:
                v = vmod(g)
                gi_ = g % 2
                P.op("sp", lambda e: e.dma_start(out=xg[gi_][:, :, :n], in_=xs[:, :, t0:t0 + n].rearrange("c p t -> p c t")),
                     reads=[B("xs", c) for c in range(16)], writes=[B("xg", gi_)], dma=f"xg{gi_}")
                for c in range(16):
                    i2 = ctr[0] % 4
                    ctr[0] += 1
                    P.op("dve" if c % 4 == 3 else "pool", lambda e: e.tensor_tensor(out=sqb[i2][:, :n], in0=xg[gi_][:, c, :n], in1=xg[gi_][:, c, :n], op=ALU.mult),
                         reads=[B("xg", gi_)], writes=[B("sq", i2)])
                    P.op("pe", lambda e: e.matmul(ps[7][:, :n], lhsT=ones_b[:], rhs=sqb[i2][:, :n], start=(c == 0), stop=(c == 15)),
                         reads=[B("sq", i2), B("ones")], writes=[PS(7)])
                P.op("act", lambda e: e.activation(out=rstd[:, :n], in_=ps[7][:, :n], func=AF.Sqrt, scale=1.0 / D, bias=eps_t[:]),
                     reads=[PS(7)], writes=[B("rstd")])
                P.op("dve", lambda e: e.reciprocal(out=rstd[:, :n], in_=rstd[:, :n]), reads=[B("rstd")], writes=[B("rstd")])
                for c in range(16):
                    i2 = ctr[1] % 2
                    ctr[1] += 1
                    P.op("dve", lambda e: e.tensor_tensor(out=tmp[i2][:, :n], in0=xg[gi_][:, c, :n], in1=rstd[:, :n], op=ALU.mult),
                         reads=[B("xg", gi_), B("rstd")], writes=[B("tmp", i2)])
                    P.op("act", lambda e: e.activation(out=H[:, c, t0:t0 + n], in_=tmp[i2][:, :n], func=AF.Identity,
                                                       scale=gmod[:, 0, c, v:v + 1], bias=mod[:, c, v:v + 1]),
                         reads=[B("tmp", i2), bgmod, bmod], writes=[B("BIG", g)])
            P.barrier()

            A = Carver(arena, S0, ARENA_BYTES)
            wsl = [A.take([128, 16, 512], BF16) for _ in range(4)]
            rpc = Carver(arena, S0 + 16384, S0 + 32768)
            rp = [rpc.take([128, 2, 512], F32) for _ in range(4)]
            rctr = [0]
            t1 = [A.take([128, 512], F32) for _ in range(2)]
            t2 = [A.take([128, 512], F32) for _ in range(2)]
            ot = [A.take([128, 512], BF16) for _ in range(2)]
            qtmp = [A.take([128, 512], BF16) for _ in range(2)]
            wi = wview(w_in[l])
            steps = [("q", 0), ("q", 1), ("kv", 0), ("u", 0), ("u", 1)]

            def b_load(t):
                kind, i = steps[t]
                sa, sbb = 2 * (t % 2), 2 * (t % 2) + 1
                if kind == "q":
                    srcA, srcB = wi[:, :, i * 512:(i + 1) * 512], None
                elif kind == "kv":
                    srcA, srcB = wi[:, :, 1024:1536], None
                else:
                    srcA, srcB = wi[:, :, 1536 + i * 512:2048 + i * 512], wi[:, :, 2560 + i * 512:3072 + i * 512]
                P.op("pool", lambda e: e.dma_start(out=wsl[sa], in_=srcA), writes=[B("ws", sa)], dma=f"ws{sa}")
                if srcB is not None:
                    P.op("pool", lambda e: e.dma_start(out=wsl[sbb], in_=srcB),
                         writes=[B("ws", sbb)] + ([B("rp", k) for k in range(4)] if sbb == 1 else []), dma=f"ws{sbb}")

            bctr = [0, 0]
            b_load(0)
            for t, (kind, i) in enumerate(steps):
                if t + 1 < len(steps):
                    b_load(t + 1)
                sa, sbb = 2 * (t % 2), 2 * (t % 2) + 1
                nj = 2 if kind == "kv" else 4

                def rope_tail(u):
                    j, g, t0, n, pa, pb, r, rq = u
                    P.op("pe", lambda e: e.matmul(ps[pb][:, :n], lhsT=perm_b[:], rhs=qtmp[r][:, :n], start=True, stop=True),
                         reads=[B("qtmp", r), B("ones")], writes=[PS(pb)])
                    P.op("dve", lambda e: e.tensor_tensor(out=t1[r][:, :n], in0=ps[pa][:, :n], in1=rp[rq][:, 0, :n], op=ALU.mult),
                         reads=[PS(pa), B("rp", rq)], writes=[B("t1", r)])
                    P.op("dve", lambda e: e.tensor_tensor(out=t2[r][:, :n], in0=ps[pb][:, :n], in1=rp[rq][:, 1, :n], op=ALU.mult),
                         reads=[PS(pb), B("rp", rq)], writes=[B("t2", r)])
                    if kind == "q":
                        h = 4 * i + j
                        P.op("dve", lambda e: e.tensor_tensor(out=ot[r][:, :n], in0=t1[r][:, :n], in1=t2[r][:, :n], op=ALU.add),
                             reads=[B("t1", r), B("t2", r)], writes=[B("ot", r)])
                        P.op("sp", lambda e: e.dma_start(out=qs[h, :, t0:t0 + n], in_=ot[r][:, :n]),
                             reads=[B("ot", r)], writes=[B("qs", g)], dma=f"ot{r}")
                    else:
                        P.op("dve", lambda e: e.tensor_tensor(out=KT[:, j, t0:t0 + n], in0=t1[r][:, :n], in1=t2[r][:, :n], op=ALU.add),
                             reads=[B("t1", r), B("t2", r)], writes=[B("KT", g)])

                pending = None
                for j in range(nj):
                    for g, t0, n in GAB:
                        pa = 2 * (bctr[0] % 3)
                        pb = pa + 1
                        bctr[0] += 1
                        r = bctr[1] % 2
                        bctr[1] += 1

                        def mmA(e):
                            for kc in range(16):
                                ins = e.matmul(ps[pa][:, :n], lhsT=wsl[sa][:, kc, j * 128:(j + 1) * 128], rhs=H[:, kc, t0:t0 + n],
                                               start=(kc == 0), stop=(kc == 15))
                            return ins
                        P.op("pe", mmA, reads=[B("ws", sa), B("BIG", g)], writes=[PS(pa)])
                        if kind in ("q", "kv"):
                            rq = rctr[0] % 4
                            rctr[0] += 1
                            P.op("sp", lambda e: e.dma_start(out=rp[rq][:, :, :n], in_=rope_d[:, :, t0:t0 + n]),
                                 reads=[B("ws", 1)], writes=[B("rp", rq)], dma=f"rp{rq}")
                            P.op("act", lambda e: e.activation(out=qtmp[r][:, :n], in_=ps[pa][:, :n], func=AF.Copy),
                                 reads=[PS(pa)], writes=[B("qtmp", r)])
                            if pending is not None:
                                rope_tail(pending)
                            pending = (j, g, t0, n, pa, pb, r, rq)
                        else:
                            def mmB(e):
                                for kc in range(16):
                                    ins = e.matmul(ps[pb][:, :n], lhsT=wsl[sbb][:, kc, j * 128:(j + 1) * 128], rhs=H[:, kc, t0:t0 + n],
                                                   start=(kc == 0), stop=(kc == 15))
                                return ins
                            P.op("pe", mmB, reads=[B("ws", sbb), B("BIG", g)], writes=[PS(pb)])
                            c = 4 * i + j
                            col0 = 2608 if g == 5 else 16 + t0
                            P.op("act", lambda e: e.activation(out=t2[r][:, :n], in_=ps[pb][:, :n], func=AF.Sigmoid),
                                 reads=[PS(pb)], writes=[B("t2", r)])
                            P.op("dve", lambda e: e.tensor_tensor(out=ot[r][:, :n], in0=ps[pa][:, :n], in1=t2[r][:, :n], op=ALU.mult),
                                 reads=[PS(pa), B("t2", r)], writes=[B("ot", r)])
                            P.op("sp", lambda e: e.dma_start(out=hcs[c, :, col0:col0 + n], in_=ot[r][:, :n]),
                                 reads=[B("ot", r)], writes=[B("hcs", g)], dma=f"ot{r}")
                if pending is not None:
                    rope_tail(pending)
                if kind == "kv":
                    for bix, blk in enumerate(list(range(16 + nAB // 128)) + [20, 21]):
                        pv_ = 6 + bix % 2

                        def mmV(e, blk=blk, pv_=pv_, sa=sa):
                            for kc in range(16):
                                ins = e.matmul(ps[pv_][:, 0:256], lhsT=H[:, kc, blk * 128:(blk + 1) * 128], rhs=wsl[sa][:, kc, 256:512],
                                               start=(kc == 0), stop=(kc == 15))
                            return ins
                        P.op("pe", mmV, reads=[B("ws", sa), B("BIG", gof(blk * 128))], writes=[PS(pv_)])
                        P.op("act", lambda e, blk=blk, pv_=pv_: e.activation(out=V[:, blk, :], in_=ps[pv_][:, 0:256], func=AF.Copy),
                             reads=[PS(pv_)], writes=[B("V", blk)])
            P.barrier()

            A = Carver(arena, S0, ARENA_BYTES)
            qg = [A.take([128, 8, 512], BF16) for _ in range(2)]
            pt = [A.take([128, 5, 512], BF16) for _ in range(3)]
            rc = [A.take([128, 512], F32) for _ in range(2)]
            if l == 0:
                feed_begin(0, 32, 52, [A.take([128, 16, 128], BF16) for _ in range(2)], 7)
            cctr = [0, 0, 0]
            pending_c = None

            def attn_tail(g, t0, bi, kvh, keys, pi):
                pvb = 3 + cctr[2] % 2
                dnb = 5 + cctr[2] % 2
                ri = cctr[2] % 2
                cctr[2] += 1
                nk = len(keys)

                def mmPV(e):
                    for idx, (kb, m) in enumerate(keys):
                        ins = e.matmul(ps[pvb][:], lhsT=V[:, kb, kvh * 128:(kvh + 1) * 128], rhs=pt[pi][:, idx, :],
                                       start=(idx == 0), stop=(idx == nk - 1))
                    return ins

                def mmDN(e):
                    for idx in range(nk):
                        e.matmul(ps[dnb][:], lhsT=ones_b[:], rhs=pt[pi][:, idx, :], start=(idx == 0), stop=False)
                    o = kvh * 512
                    return e.matmul(ps[dnb][:], lhsT=ones_b[0:1, :], rhs=esink[0:1, o:o + 512], start=False, stop=True)
                P.op("pe", mmPV, reads=[B("pt", pi)] + [B("V", kb) for kb, _ in keys], writes=[PS(pvb)])
                P.op("pe", mmDN, reads=[B("pt", pi), B("ones"), B("esink")], writes=[PS(dnb)])
                P.op("dve", lambda e: e.reciprocal(out=rc[ri][:], in_=ps[dnb][:]), reads=[PS(dnb)], writes=[B("rc", ri)])
                tb = t0 + bi * 128
                P.op("dve", lambda e: e.tensor_tensor(
                    out=H[:, 4 * kvh:4 * kvh + 4, tb:tb + 128], in0=ps[pvb][:].rearrange("p (h r) -> p h r", h=4),
                    in1=rc[ri][:].rearrange("p (h r) -> p h r", h=4), op=ALU.mult),
                    reads=[PS(pvb), B("rc", ri)], writes=[B("BIG", g)])

            for gi, (g, t0, n) in enumerate(GCD):
                qi = gi % 2
                P.op("sp", lambda e: e.dma_start(out=qg[qi][:, :, :n], in_=qs[:, :, t0:t0 + n].rearrange("h p t -> p h t")),
                     reads=[B("qs", g)], writes=[B("qg", qi)], dma=f"qg{qi}")
                for bi in range(n // 128):
                    blk = t0 // 128 + bi
                    for kvh in range(2):
                        if g == 5:
                            keys = [(20, None), (21, None)]
                        else:
                            keys = []
                            if blk - 1 >= 0:
                                keys.append((blk - 1, 0))
                            keys.append((blk, None))
                            if blk + 1 <= kb_max:
                                keys.append((blk + 1, 1))
                            keys += [(20, None), (21, None)]
                        pi = cctr[0] % 3
                        cctr[0] += 1
                        for idx, (kb, m) in enumerate(keys):
                            sbk = cctr[1] % 3
                            cctr[1] += 1

                            def mmS(e):
                                ins = e.matmul(ps[sbk][:].rearrange("p (h r) -> p h r", h=4), lhsT=KT[:, kvh, kb * 128:(kb + 1) * 128],
                                               rhs=qg[qi][:, 4 * kvh:4 * kvh + 4, bi * 128:(bi + 1) * 128], start=True, stop=(m is None))
                                if m is not None:
                                    ins = e.matmul(ps[sbk][:], lhsT=ident_b[:], rhs=mask_b[:, m, :], start=False, stop=True)
                                return ins
                            P.op("pe", mmS, reads=[B("KT", gof(kb * 128)), B("qg", qi), B("ones")], writes=[PS(sbk)])
                            P.op("act", lambda e: e.activation(out=pt[pi][:, idx, :], in_=ps[sbk][:], func=AF.Exp, scale=SCALE),
                                 reads=[PS(sbk)], writes=[B("pt", pi)])
                        if pending_c is not None:
                            attn_tail(*pending_c)
                        pending_c = (g, t0, bi, kvh, keys, pi)
                        if l == 0:
                            feed_step()
            if pending_c is not None:
                attn_tail(*pending_c)
            if l == 0:
                feed_finish()
            P.barrier()

            A = Carver(arena, S0, ARENA_BYTES)
            RT = [A.take([128, 4, 544], BF16) for _ in range(3)]
            accs = [A.take([128, 8, 512], F32) for _ in range(2)]
            cw = A.take([128, 8192], BF16)
            for q4 in range(4):
                P.op("pool", lambda e: e.dma_start(out=cw[:, q4 * 2048:(q4 + 1) * 2048],
                                                   in_=convw_d[:, l * 8192 + q4 * 2048:l * 8192 + (q4 + 1) * 2048]),
                     writes=[B("cw", q4)], dma=f"ws{q4}")
            csq = [A.take([128, 512], BF16) for _ in range(2)]
            accb = [A.take([128, 512], BF16) for _ in range(2)]
            mean = A.take([128, 512], F32)
            crs = A.take([128, 512], F32)
            cctr = [0, 0, 0]

            pend_stats = []
            for gi, (g, t0, n) in enumerate(GCD):
                col0 = 2608 if g == 5 else 16 + t0
                acc = accs[gi % 2]
                sb_, qb_ = (3, 4) if gi % 2 == 0 else (5, 6)
                for c in range(8):
                    cp = convp[:, l * 8 + c, :]
                    di = cctr[0] % 3
                    cctr[0] += 1
                    cb = cctr[1] % 3
                    cctr[1] += 1

                    def ldR(e):
                        src = hcs[c].rearrange("(cbi cc) t -> cc cbi t", cc=32)
                        for j in range(4):
                            ins = e.dma_start(out=RT[di][j * 32:(j + 1) * 32, :, 0:n + 28],
                                              in_=src[:, :, col0 - 15 + j:col0 - 15 + j + n + 28])
                        return ins
                    P.op("sp", ldR, reads=[B("hcs", gg) for gg in range(6)] + [B("hcs_pad")], writes=[B("RT", di)], dma=f"dg{di}")

                    def mmC(e):
                        for q in range(8):
                            for cbi in range(4):
                                wo_ = ((c * 8 + q) * 4 + cbi) * 32
                                ins = e.matmul(ps[cb][cbi * 32:(cbi + 1) * 32, :n], lhsT=cw[:, wo_:wo_ + 32], rhs=RT[di][:, cbi, 4 * q:4 * q + n],
                                               start=(q == 0), stop=(q == 7), tile_position=(0, cbi * 32))
                        return ins
                    P.op("pe", mmC, reads=[B("RT", di)] + [B("cw", q4) for q4 in range(4)], writes=[PS(cb)])
                    P.op("act", lambda e: e.activation(out=acc[:, c, :n], in_=ps[cb][:, :n], func=AF.Identity, bias=cp[:, 31:32]),
                         reads=[PS(cb), B("convp")], writes=[B("acc", gi % 2, c)])
                    i2 = cctr[2] % 2
                    cctr[2] += 1
                    P.op("act", lambda e: e.activation(out=csq[i2][:, :n], in_=acc[:, c, :n], func=AF.Square),
                         reads=[B("acc", gi % 2, c)], writes=[B("csq", i2)])
                    P.op("dve", lambda e: e.tensor_copy(out=accb[i2][:, :n], in_=acc[:, c, :n]),
                         reads=[B("acc", gi % 2, c)], writes=[B("accb", i2)])
                    def stats_mm(c=c, i2=i2, n=n, sb_=sb_, qb_=qb_):
                        P.op("pe", lambda e: e.matmul(ps[sb_][:, :n], lhsT=ones_b[:], rhs=accb[i2][:, :n], start=(c == 0), stop=(c == 7)),
                             reads=[B("accb", i2), B("ones")], writes=[PS(sb_)])
                        P.op("pe", lambda e: e.matmul(ps[qb_][:, :n], lhsT=ones_b[:], rhs=csq[i2][:, :n], start=(c == 0), stop=(c == 7)),
                             reads=[B("csq", i2), B("ones")], writes=[PS(qb_)])
                    pend_stats.append(stats_mm)
                    if len(pend_stats) > 1:
                        pend_stats.pop(0)()
                while pend_stats:
                    pend_stats.pop(0)()
                P.op("act", lambda e: e.activation(out=mean[:, :n], in_=ps[sb_][:, :n], func=AF.Identity, scale=1.0 / 1024),
                     reads=[PS(sb_)], writes=[B("mean")])
                P.op("dve", lambda e: e.tensor_tensor(out=crs[:, :n], in0=mean[:, :n], in1=mean[:, :n], op=ALU.mult),
                     reads=[B("mean")], writes=[B("crs")])
                P.op("dve", lambda e: e.scalar_tensor_tensor(out=crs[:, :n], in0=ps[qb_][:, :n], scalar=1.0 / 1024, in1=crs[:, :n],
                                                             op0=ALU.mult, op1=ALU.subtract),
                     reads=[PS(qb_), B("crs")], writes=[B("crs")])
                P.op("act", lambda e: e.activation(out=crs[:, :n], in_=crs[:, :n], func=AF.Sqrt, bias=eps_t[:]), reads=[B("crs")], writes=[B("crs")])
                P.op("dve", lambda e: e.reciprocal(out=crs[:, :n], in_=crs[:, :n]), reads=[B("crs")], writes=[B("crs")])
                for c in range(8):
                    cp = convp[:, l * 8 + c, :]
                    P.op("dve", lambda e: e.tensor_tensor(out=acc[:, c, :n], in0=acc[:, c, :n], in1=mean[:, :n], op=ALU.subtract),
                         reads=[B("acc", gi % 2, c), B("mean")], writes=[B("acc", gi % 2, c)])
                    P.op("dve", lambda e: e.tensor_tensor(out=acc[:, c, :n], in0=acc[:, c, :n], in1=crs[:, :n], op=ALU.mult),
                         reads=[B("acc", gi % 2, c), B("crs")], writes=[B("acc", gi % 2, c)])
                    P.op("act", lambda e: e.activation(out=H[:, 8 + c, t0:t0 + n], in_=acc[:, c, :n], func=AF.Silu,
                                                       scale=cp[:, 32:33], bias=cp[:, 33:34]),
                         reads=[B("acc", gi % 2, c), B("convp")], writes=[B("BIG", g)])
            P.barrier()

            A = Carver(arena, S0, ARENA_BYTES)
            wsl = [A.take([128, 16, 512], BF16) for _ in range(2)]
            xt = [A.take([128, 512], F32) for _ in range(8)]
            wo = wview(w_out[l])
            NPB = 8
            if l == 0:
                NPB = 7
                feed_begin(0, 52, 96, [A.take([128, 16, 128], BF16) for _ in range(2)], 7)

            def d_load(s):
                i = s % 2
                P.op("pool", lambda e: e.dma_start(out=wsl[i], in_=wo[:, :, s * 512:(s + 1) * 512]), writes=[B("ws", i)], dma=f"ws{i}")
            d_load(0)
            units = [(s_, j, g, t0, n) for s_ in range(4) for j in range(4) for (g, t0, n) in GCD]

            def x_load(u):
                s_, j, g, t0, n = units[u]
                dc = 4 * s_ + j
                xi = u % 8
                P.op("sp", lambda e: e.dma_start(out=xt[xi][:, :n], in_=xs[dc, :, t0:t0 + n]),
                     reads=[B("xs", dc)], writes=[B("xt", xi)], dma=f"xt{xi}")
            for u0 in range(4):
                x_load(u0)
            for u, (s_, j, g, t0, n) in enumerate(units):
                if j == 0 and g == 0 and s_ + 1 < 4:
                    d_load(s_ + 1)
                if u + 4 < len(units):
                    x_load(u + 4)
                i = s_ % 2
                dc = 4 * s_ + j
                v = vmod(g)
                xi = u % 8
                pb = u % NPB
                if l == 0:
                    feed_step()

                def mmO(e):
                    for kc in range(16):
                        ins = e.matmul(ps[pb][:, :n], lhsT=wsl[i][:, kc, j * 128:(j + 1) * 128], rhs=H[:, kc, t0:t0 + n],
                                       start=(kc == 0), stop=(kc == 15))
                    return ins
                P.op("pe", mmO, reads=[B("ws", i), B("BIG", g)], writes=[PS(pb)])
                P.op("dve", lambda e: e.scalar_tensor_tensor(
                    out=xt[xi][:, :n], in0=ps[pb][:, :n], scalar=mod[:, 32 + dc, v:v + 1], in1=xt[xi][:, :n], op0=ALU.mult, op1=ALU.add),
                    reads=[PS(pb), B("xt", xi), bmod], writes=[B("xt", xi)])
                P.op("sp", lambda e: e.dma_start(out=xs[dc, :, t0:t0 + n], in_=xt[xi][:, :n]),
                     reads=[B("xt", xi)], writes=[B("xsD", dc, g)], dma=f"xo{xi}")
            if l == 0:
                feed_finish()
                ada_gmod(0, parts=(1,))
            P.barrier()

            XACC = Carver(arena, 0, 65536).take([128, 16, 1024], F32)
            H2 = Carver(arena, 65536, 98304).take([128, 16, 1024], BF16)
            A = Carver(arena, 98304, ARENA_BYTES)
            W1 = [A.take([128, 16, 512], BF16) for _ in range(2)]
            W2 = [A.take([128, 4, 2048], BF16) for _ in range(2)]
            hid = [A.take([128, 4, 512], BF16) for _ in range(2)]
            rt = [A.take([128, 512], BF16) for _ in range(2)]
            sq = [A.take([128, 512], BF16) for _ in range(2)]
            rstd = A.take([128, 1024], F32)
            tmpn = A.take([128, 1024], F32)
            tail = Carver(arena, A.off, ARENA_BYTES)
            if last:
                yt = [tail.take([128, 512], F32) for _ in range(2)]
            else:
                wa = [tail.take([128, 16, 128], BF16) for _ in range(2)]
                wav = wview(w_ada[l + 1])
            w1v = wview(w1_d[l])
            w2v = w2_d[l].rearrange("(hs p) c -> p hs c", p=128)
            mctr = [0, 0, 0, 0, 0]
            def mk_subs(ranges):
                subs_, o_ = [], 0
                for (lo, hi, v_) in ranges:
                    m_ = hi - lo
                    parts = [m_] if m_ <= 512 else [m_ // 2, m_ - m_ // 2]
                    c_ = lo
                    for p_ in parts:
                        subs_.append((o_, p_, v_, c_))
                        o_ += p_
                        c_ += p_
                assert o_ <= 1024
                return subs_
            if last:
                sglist = [(0, mk_subs([(0, 1024, 0)])), (1024, mk_subs([(1024, 2048, 0)]))]
            else:
                lat = 2048 + nCD
                a_ = min(1024, -(-(lat + 256) // 3 // 64) * 64)
                sglist = [(0, mk_subs([(0, a_, 0)])), (a_, mk_subs([(a_, 2 * a_, 0)])),
                          (2 * a_, mk_subs([(2 * a_, lat, 0), (2560, 2816, 1)]))]
                assert sum(len(sg[1]) for sg in sglist) * 16 >= 96
                assert len(sglist[0][1]) == 2 and len(sglist[1][1]) == 2
            sbank = [6, 7, 5]

            def wa_load(oc):
                i = oc % 2
                P.op("pool", lambda e: e.dma_start(out=wa[i], in_=wav[:, :, oc * 128:(oc + 1) * 128]), writes=[B("wa", i)], dma=f"wa{i}")
            unit = 0
            if not last:
                wa_load(0)
            preloaded = [False]

            def emit_ld(c, subs_):
                def ld(e):
                    for (o, n, v, col) in subs_:
                        ins = e.dma_start(out=XACC[:, c, o:o + n], in_=xs[c, :, col:col + n])
                    return ins
                P.op("sp", ld, reads=[B("xs", c)], writes=[B("XACC", c)], dma=f"xacc{c}")

            for sgi, (s0, subs) in enumerate(sglist):

                def m_load(hsg):
                    i = hsg % 2
                    P.op("pool", lambda e: e.dma_start(out=W1[i], in_=w1v[:, :, hsg * 512:(hsg + 1) * 512]), writes=[B("W1", i)], dma=f"W1{i}")
                    P.op("pool", lambda e: e.dma_start(out=W2[i], in_=w2v[:, hsg * 4:(hsg + 1) * 4, :]), writes=[B("W2", i)], dma=f"W2{i}")
                m_load(0)

                def stats(final):
                    for c in range(16):
                        if not final and not preloaded[0]:
                            emit_ld(c, subs)
                        for k, (o, n, v, col) in enumerate(subs):
                            i2 = k % 2
                            if k % 2 == 0:
                                P.op("act", lambda e: e.activation(out=sq[i2][:, :n], in_=XACC[:, c, o:o + n], func=AF.Square),
                                     reads=[B("XACC", c)], writes=[B("sq", i2)])
                            else:
                                P.op("dve", lambda e: e.tensor_tensor(out=sq[i2][:, :n], in0=XACC[:, c, o:o + n], in1=XACC[:, c, o:o + n], op=ALU.mult),
                                     reads=[B("XACC", c)], writes=[B("sq", i2)])
                            P.op("pe", lambda e: e.matmul(ps[sbank[k]][:, :n], lhsT=ones_b[:], rhs=sq[i2][:, :n], start=(c == 0), stop=(c == 15)),
                                 reads=[B("sq", i2), B("ones")], writes=[PS(sbank[k])])
                    for k, (o, n, v, col) in enumerate(subs):
                        P.op("act", lambda e: e.activation(out=rstd[:, o:o + n], in_=ps[sbank[k]][:, :n], func=AF.Sqrt, scale=1.0 / D, bias=eps_t[:]),
                             reads=[PS(sbank[k])], writes=[B("rstd")])
                        P.op("dve", lambda e: e.reciprocal(out=rstd[:, o:o + n], in_=rstd[:, o:o + n]), reads=[B("rstd")], writes=[B("rstd")])

                stats(False)
                preloaded[0] = False
                for c in range(16):
                    for k, (o, n, v, col) in enumerate(subs):
                        P.op("dve", lambda e: e.tensor_tensor(out=tmpn[:, o:o + n], in0=XACC[:, c, o:o + n], in1=rstd[:, o:o + n], op=ALU.mult),
                             reads=[B("XACC", c), B("rstd")], writes=[B("tmpn", k)])
                        P.op("act", lambda e: e.activation(out=H2[:, c, o:o + n], in_=tmpn[:, o:o + n], func=AF.Identity,
                                                           scale=gmod[:, 1, c, v:v + 1], bias=mod[:, 48 + c, v:v + 1]),
                             reads=[B("tmpn", k), bgmod, bmod], writes=[B("H2", k)])
                def stage2(hsg, k, hi):
                    o, n, v, col = subs[k]
                    wi_ = hsg % 2
                    for dc in range(16):
                        yb = 3 + mctr[4] % 3
                        mctr[4] += 1

                        def mm2(e):
                            for j in range(4):
                                ins = e.matmul(ps[yb][:, :n], lhsT=W2[wi_][:, j, dc * 128:(dc + 1) * 128], rhs=hid[hi][:, j, :n],
                                               start=(j == 0), stop=(j == 3))
                            return ins
                        P.op("pe", mm2, reads=[B("W2", wi_), B("hid", hi)], writes=[PS(yb)])
                        P.op("dve", lambda e: e.scalar_tensor_tensor(
                            out=XACC[:, dc, o:o + n], in0=ps[yb][:, :n], scalar=mod[:, 80 + dc, v:v + 1], in1=XACC[:, dc, o:o + n],
                            op0=ALU.mult, op1=ALU.add),
                            reads=[PS(yb), B("XACC", dc), bmod], writes=[B("XACC", dc)])

                pend2 = None
                for hsg in range(16):
                    wi_ = hsg % 2
                    for k, (o, n, v, col) in enumerate(subs):
                        if not last and unit < 96:
                            oc = unit
                            unit += 1
                            if oc + 1 < 96:
                                wa_load(oc + 1)
                            ada_mm(l + 1, oc, wa[oc % 2], 0, B("wa", oc % 2), 7)
                            if oc % 4 == 3:
                                ada_evac(l + 1, oc // 4, 7)
                        hi = mctr[1] % 2
                        mctr[1] += 1
                        for j in range(4):
                            hb = mctr[2] % 3
                            mctr[2] += 1
                            ri = mctr[3] % 2
                            mctr[3] += 1

                            def mm1(e):
                                for kc in range(16):
                                    ins = e.matmul(ps[hb][:, :n], lhsT=W1[wi_][:, kc, j * 128:(j + 1) * 128], rhs=H2[:, kc, o:o + n],
                                                   start=(kc == 0), stop=(kc == 15))
                                return ins
                            P.op("pe", mm1, reads=[B("W1", wi_), B("H2", k)], writes=[PS(hb)])
                            P.op("act", lambda e: e.activation(out=rt[ri][:, :n], in_=ps[hb][:, :n], func=AF.Relu),
                                 reads=[PS(hb)], writes=[B("rt", ri)])
                            P.op("dve", lambda e: e.tensor_tensor(out=hid[hi][:, j, :n], in0=ps[hb][:, :n], in1=rt[ri][:, :n], op=ALU.mult),
                                 reads=[PS(hb), B("rt", ri)], writes=[B("hid", hi)])
                        if pend2 is not None:
                            stage2(*pend2)
                        pend2 = (hsg, k, hi)
                        if k == 0 and hsg + 1 < 16:
                            m_load(hsg + 1)
                if pend2 is not None:
                    stage2(*pend2)
                if not last:
                    nxt = sglist[sgi + 1][1] if sgi + 1 < len(sglist) else None
                    LAG = 8
                    for c in range(16):
                        def stx(e):
                            for (o, n, v, col) in subs:
                                ins = e.dma_start(out=xs[c, :, col:col + n], in_=XACC[:, c, o:o + n])
                            return ins
                        P.op("sp", stx, reads=[B("XACC", c)], writes=[B("xs", c)], dma=f"xst{c}")
                        if nxt is not None and c >= LAG:
                            emit_ld(c - LAG, nxt)
                    if nxt is not None:
                        for c in range(16 - LAG, 16):
                            emit_ld(c, nxt)
                        preloaded[0] = True
                else:
                    stats(True)
                    for c in range(16):
                        for k, (o, n, v, col) in enumerate(subs):
                            yi = mctr[0] % 2
                            mctr[0] += 1
                            P.op("dve", lambda e: e.tensor_tensor(out=tmpn[:, o:o + n], in0=XACC[:, c, o:o + n], in1=rstd[:, o:o + n], op=ALU.mult),
                                 reads=[B("XACC", c), B("rstd")], writes=[B("tmpn", k)])
                            P.op("act", lambda e: e.activation(out=yt[yi][:, :n], in_=tmpn[:, o:o + n], func=AF.Identity,
                                                               scale=gvec[:, 2 * L, c:c + 1]),
                                 reads=[B("tmpn", k), B("gvec")], writes=[B("yt", yi)])
                            P.op("sp", lambda e: e.dma_start(out=y_d[c, :, col:col + n], in_=yt[yi][:, :n]),
                                 reads=[B("yt", yi)], writes=[B("yout")], dma=f"yo{yi}")
            if not last:
                assert unit == 96
                ada_gmod(l + 1)
            P.barrier()
        P.wait_all("sp", [B("yout")])
        P.emit()
    return nc


def _rope_tables(pos):
    pos = np.asarray(pos, dtype=np.int64)
    row = (pos // 64).astype(np.float32)
    col = (pos % 64).astype(np.float32)
    n_freq = 32
    inv = (np.float32(10000.0) ** (-np.arange(n_freq, dtype=np.float32) / np.float32(n_freq))).astype(np.float32)
    theta = np.concatenate([row[:, None] * inv, col[:, None] * inv], axis=-1)
    theta = np.concatenate([theta, theta], axis=-1)
    return np.cos(theta).astype(np.float32), np.sin(theta).astype(np.float32)


def _fm(a2d):
    t = a2d.shape[0]
    return np.ascontiguousarray(a2d.T.reshape(16, 128, t))


def _vec_fm(v, nch):
    v = np.asarray(v)
    lead = v.shape[:-1]
    r = v.reshape(lead + (nch, 128))
    return np.ascontiguousarray(np.moveaxis(r, -1, 0))


def prepare_shared(inp, L):
    f = lambda a: np.ascontiguousarray(np.asarray(a, dtype=np.float32))
    sh = {}
    w_in = np.asarray(inp["w_in"])[:L]
    sh["w_in"] = f(w_in)
    sh["w_ada"] = f(np.asarray(inp["w_ada"])[:L])
    b = _vec_fm(np.asarray(inp["b_ada"])[:L], 96)
    sh["b_ada2"] = np.ascontiguousarray(np.repeat(b[:, :, :, None], 2, axis=3).reshape(128, L * 96 * 2))
    gv = np.concatenate([np.asarray(inp["g_mix"])[:L], np.asarray(inp["g_mlp"])[:L], np.asarray(inp["g_final"])[None]], axis=0)
    sh["gvec"] = np.ascontiguousarray(_vec_fm(gv, 16).reshape(128, (2 * L + 1) * 16))
    sink = np.asarray(inp["attn_sink"])[:L].reshape(L, 2, 4)
    sh["sinkrow"] = np.ascontiguousarray(np.repeat(sink[:, :, :, None], 128, axis=3).reshape(1, L * 2 * 512)).astype(np.float32)
    sh["w_out"] = f(np.asarray(inp["w_out"])[:L])
    sh["w1"] = f(np.asarray(inp["w_mlp1"])[:L])
    sh["w2"] = f(np.asarray(inp["w_mlp2"])[:L])
    ident = np.eye(128, dtype=np.float32)
    j = np.arange(128)[:, None]
    r = np.arange(128)[None, :]
    mL = np.where(j >= r, 0.0, -30000.0).astype(np.float32)
    mR = np.where(j <= r, 0.0, -30000.0).astype(np.float32)
    sh["cst"] = np.ascontiguousarray(np.concatenate([ident, np.tile(mL, (1, 4)), np.tile(mR, (1, 4))], axis=1))
    cw = np.asarray(inp["conv_w"])[:L]
    sh["_conv"] = (cw, np.asarray(inp["conv_b"])[:L], np.asarray(inp["conv_ln_g"])[:L], np.asarray(inp["conv_ln_b"])[:L])
    return sh


def prepare_core(inp, sh, core, L):
    b, half = core // 2, core % 2
    x = np.asarray(inp["x"])[b]
    if half == 0:
        pos = np.arange(0, NLAT)
    else:
        pos = np.arange(4095, 4095 - NLAT, -1)
    xl = x[pos]
    ctx = np.asarray(inp["ctx"])[b]
    m = {k: v for k, v in sh.items() if not k.startswith("_")}
    m["xin"] = _fm(np.concatenate([xl, ctx], axis=0))
    cc = np.stack([np.asarray(inp["c"])[b], np.asarray(inp["c_ctx"])], axis=-1)
    m["cT"] = np.ascontiguousarray(cc.reshape(16, 128, 2).transpose(1, 0, 2)).astype(np.float32)
    cos, sin = _rope_tables(pos)
    sgn = np.concatenate([-np.ones(64, np.float32), np.ones(64, np.float32)])
    cosT = np.concatenate([cos.T, np.ones((128, NCTX), np.float32)], axis=1)
    sinT = np.concatenate([(sin * sgn[None, :]).T, np.zeros((128, NCTX), np.float32)], axis=1)
    m["rope"] = np.ascontiguousarray(np.stack([cosT, sinT], axis=1)).astype(np.float32)
    cw, cb, lg, lb = sh["_conv"]
    if half == 1:
        cw = cw[:, ::-1, :]
    taps = np.moveaxis(cw.reshape(L, 31, 8, 128), (3, 0, 2, 1), (0, 1, 2, 3))
    extra = np.stack([_vec_fm(cb, 8), _vec_fm(lg, 8), _vec_fm(lb, 8)], axis=-1)
    m["convp"] = np.ascontiguousarray(np.concatenate([taps, extra], axis=-1).reshape(128, L * 8 * 34)).astype(np.float32)
    wp = np.concatenate([cw, np.zeros((L, 1, 1024), cw.dtype)], axis=1).reshape(L, 8, 4, 8, 4, 32)
    T = np.zeros((4, 32, L, 8, 8, 4, 32), np.float32)
    for c_ in range(32):
        T[:, c_, :, :, :, :, c_] = wp[..., c_].transpose(2, 0, 3, 1, 4)
    m["convw"] = np.ascontiguousarray(T.reshape(128, L * 8192))
    return m


def assemble(results, n_cores_run, cores):
    out = np.zeros((4, 4096, D), np.float32)
    for core, res in zip(cores, results):
        b, half = core // 2, core % 2
        yt = res["y"].reshape(D, NOWN).T
        if half == 0:
            out[b, 0:NOWN] = yt
        else:
            out[b, 4095 - np.arange(NOWN)] = yt
    return out


_CACHE = {}


def kernel(**inputs):
    L = 4
    if L not in _CACHE:
        _CACHE[L] = build_program(L)
    nc = _CACHE[L]
    sh = prepare_shared(inputs, L)
    cores = list(range(8))
    in_maps = [prepare_core(inputs, sh, c, L) for c in cores]
    res = run_bass_kernel_spmd(nc, in_maps, core_ids=cores)
    return assemble(res.results, 8, cores)
```
